# Optimizing a Trainium2 kernel written in Bass

```python
import math
import jax
import jax.numpy as jnp
from jax import lax
import numpy as np

D_MODEL = 1024
BATCH = 16
SEQ = 2048
DEPTH = 4

GRID_W = 64
CTX_LEN = 256
HEAD_DIM = 64
DN_HEADS = 4
DN_WIDTH = DN_HEADS * HEAD_DIM
DN_CHUNK = 64
CONV_W = 5
SWA_HEADS = 8
SWA_KV_HEADS = 2
SWA_GROUP = SWA_HEADS // SWA_KV_HEADS
SWA_WIDTH = SWA_HEADS * HEAD_DIM
SWA_KV_WIDTH = SWA_KV_HEADS * HEAD_DIM
WINDOW = 128
SWA_BLOCK = 128
ROPE_BASE = 10000.0
HG_HEADS = 4
HG_WIDTH = HG_HEADS * HEAD_DIM
HG_CHUNK = 64
D_MIX = DN_WIDTH + SWA_WIDTH + HG_WIDTH
D_FF = 4 * D_MODEL
N_MOD = 6
EPS = 1e-6
F32 = jnp.float32
IN_SIZES = (3 * DN_WIDTH, DN_WIDTH, 2 * DN_HEADS, 2 * DN_HEADS,
            SWA_WIDTH, SWA_KV_WIDTH, SWA_KV_WIDTH,
            HG_WIDTH, 2 * HG_WIDTH, HG_WIDTH, HG_WIDTH)
D_IN = 4 * DN_WIDTH + 4 * DN_HEADS + SWA_WIDTH + 2 * SWA_KV_WIDTH + 5 * HG_WIDTH

kernel_name = 'hybrid_dit_deltanet_swa_hgrn2'


def rmsnorm(x, gain):
    xf = x.astype(F32)
    y = xf * lax.rsqrt(jnp.mean(xf * xf, axis=-1, keepdims=True) + EPS)
    return (y * gain.astype(F32)).astype(x.dtype)


def modulate(h, shift, scale):
    return h * (1 + scale) + shift


def to_heads(a, n):
    B, T, _ = a.shape
    return a.reshape(B, T, n, -1).transpose(0, 2, 1, 3)


def from_heads(a):
    B, H, T, d = a.shape
    return a.transpose(0, 2, 1, 3).reshape(B, T, H * d)


def l2norm(a):
    a = a.astype(F32)
    return a * lax.rsqrt(jnp.sum(a * a, axis=-1, keepdims=True) + EPS)


def head_norm_gate(o, gain, gate):
    o = o * lax.rsqrt(jnp.mean(o * o, axis=-1, keepdims=True) + EPS) * gain.astype(F32)
    return (from_heads(o) * jax.nn.silu(gate.astype(F32))).astype(gate.dtype)


def centred_conv(a, w):
    pad = CONV_W // 2
    return lax.conv_general_dilated(a, w[:, None, :], (1,), [(pad, pad)],
                                    dimension_numbers=('NWC', 'WIO', 'NWC'),
                                    feature_group_count=a.shape[-1])


def split_in(p):
    out, start = [], 0
    for size in IN_SIZES:
        out.append(p[..., start:start + size])
        start += size
    return out


def flip_t(a, d):
    return jnp.flip(a, axis=2) if d else a


def to_chunks(a, C):
    B, H, T = a.shape[:3]
    return jnp.moveaxis(a.reshape((B, H, T // C, C) + a.shape[3:]), 2, 0)


def from_chunks(a):
    a = jnp.moveaxis(a, 0, 2)
    return a.reshape(a.shape[:2] + (-1,) + a.shape[4:])


def axial_rope_tables(T):
    rows = T // GRID_W
    row = jnp.repeat(jnp.arange(rows), GRID_W).astype(F32)
    col = jnp.tile(jnp.arange(GRID_W), rows).astype(F32)
    half = HEAD_DIM // 2
    inv = ROPE_BASE ** (-jnp.arange(0, half, 2, dtype=F32) / half)
    ang = jnp.concatenate([row[:, None] * inv, col[:, None] * inv], axis=-1)
    return jnp.cos(ang), jnp.sin(ang)


def apply_rope(a, cos, sin):
    a1, a2 = a[..., :HEAD_DIM // 2], a[..., HEAD_DIM // 2:]
    return jnp.concatenate([a1 * cos - a2 * sin, a1 * sin + a2 * cos], axis=-1)


def gated_delta_scan(q, k, v, beta, g, S0):
    C = DN_CHUNK
    tri = jnp.tril(jnp.ones((C, C), bool))
    strict = jnp.tril(jnp.ones((C, C), bool), -1)
    eye = jnp.eye(C, dtype=F32)

    def step(S, blk):
        qi, ki, vi, bi, gi = blk
        gc = jnp.cumsum(gi, axis=-1)
        decay = jnp.exp(jnp.where(tri, gc[..., :, None] - gc[..., None, :], -jnp.inf))
        kb = ki * bi[..., None]
        a = jnp.where(strict, jnp.einsum('bhid,bhjd->bhij', kb, ki) * decay, 0.0) + eye
        rhs = jnp.concatenate([vi * bi[..., None], kb * jnp.exp(gc)[..., None]], axis=-1)
        uw = lax.linalg.triangular_solve(a, rhs, left_side=True, lower=True, unit_diagonal=True)
        u, w = uw[..., :HEAD_DIM], uw[..., HEAD_DIM:]
        v_new = u - jnp.einsum('bhck,bhkv->bhcv', w, S)
        scores = jnp.einsum('bhid,bhjd->bhij', qi, ki) * decay
        o = (jnp.einsum('bhck,bhkv->bhcv', qi * jnp.exp(gc)[..., None], S)
             + jnp.einsum('bhij,bhjv->bhiv', scores, v_new))
        g_last = gc[..., -1:]
        S = (S * jnp.exp(g_last)[..., None]
             + jnp.einsum('bhck,bhcv->bhkv', ki * jnp.exp(g_last - gc)[..., None], v_new))
        return S, o

    S, o = lax.scan(step, S0, tuple(to_chunks(t, C) for t in (q, k, v, beta, g)))
    return from_chunks(o), S


def gla_scan(q, k, v, logf, S0):
    C = HG_CHUNK
    tri = jnp.tril(jnp.ones((C, C), bool))[:, :, None]

    def step(S, blk):
        qi, ki, vi, lfi = blk
        bc = jnp.cumsum(lfi, axis=2)
        decay = jnp.exp(jnp.where(tri, bc[:, :, :, None, :] - bc[:, :, None, :, :], -jnp.inf))
        scores = jnp.einsum('bhik,bhijk,bhjk->bhij', qi, decay, ki)
        o = (jnp.einsum('bhij,bhjv->bhiv', scores, vi)
             + jnp.einsum('bhck,bhkv->bhcv', qi * jnp.exp(bc), S))
        b_last = bc[:, :, -1:, :]
        S = (S * jnp.exp(b_last)[:, :, 0, :, None]
             + jnp.einsum('bhck,bhcv->bhkv', ki * jnp.exp(b_last - bc), vi))
        return S, o

    S, o = lax.scan(step, S0, tuple(to_chunks(t, C) for t in (q, k, v, logf)))
    return from_chunks(o), S


def deltanet_group(p_lat, p_ctx, conv_w, A_log, dt_bias, norm_g, need_ctx):
    def prep(qkv, a, b):
        qkv = jax.nn.silu(centred_conv(qkv, conv_w))
        q, k, v = jnp.split(qkv, 3, axis=-1)
        q = l2norm(to_heads(q, DN_HEADS)) * HEAD_DIM ** -0.5
        k = l2norm(to_heads(k, DN_HEADS))
        v = to_heads(v, DN_HEADS).astype(F32)
        B, T, _ = a.shape
        a = a.reshape(B, T, 2, DN_HEADS).astype(F32)
        g = -jnp.exp(A_log.astype(F32)) * jax.nn.softplus(a + dt_bias.astype(F32))
        beta = jax.nn.sigmoid(b.reshape(B, T, 2, DN_HEADS).astype(F32))
        return q, k, v, g.transpose(2, 0, 3, 1), beta.transpose(2, 0, 3, 1)

    qkv_l, gate_l, a_l, b_l = p_lat
    qkv_c, gate_c, a_c, b_c = p_ctx
    ql, kl, vl, gl, bl = prep(qkv_l, a_l, b_l)
    qc, kc, vc, gc, bc = prep(qkv_c, a_c, b_c)
    B, H = ql.shape[:2]
    o_l = jnp.zeros_like(vl)
    o_c = jnp.zeros_like(vc)
    for d in range(2):
        S0 = jnp.zeros((B, H, HEAD_DIM, HEAD_DIM), F32)
        oc, S_ctx = gated_delta_scan(flip_t(qc, d), flip_t(kc, d), flip_t(vc, d),
                                     flip_t(bc[d], d), flip_t(gc[d], d), S0)
        ol, _ = gated_delta_scan(flip_t(ql, d), flip_t(kl, d), flip_t(vl, d),
                                 flip_t(bl[d], d), flip_t(gl[d], d), S_ctx)
        o_l = o_l + flip_t(ol, d)
        o_c = o_c + flip_t(oc, d)
    y_l = head_norm_gate(o_l, norm_g, gate_l)
    y_c = head_norm_gate(o_c, norm_g, gate_c) if need_ctx else None
    return y_l, y_c


def swa_group(q_l, k_l, v_l, q_c, k_c, v_c, sink, cos, sin, need_ctx):
    B, T, _ = q_l.shape
    L = k_c.shape[1]
    nb, Bk = T // SWA_BLOCK, SWA_BLOCK
    scale = HEAD_DIM ** -0.5
    ql = apply_rope(to_heads(q_l, SWA_HEADS).astype(F32), cos, sin) * scale
    kl = apply_rope(to_heads(k_l, SWA_KV_HEADS).astype(F32), cos, sin)
    vl = to_heads(v_l, SWA_KV_HEADS).astype(F32)
    kc = to_heads(k_c, SWA_KV_HEADS).astype(F32)
    vc = to_heads(v_c, SWA_KV_HEADS).astype(F32)
    sink = sink.astype(F32).reshape(SWA_KV_HEADS, SWA_GROUP)
    qb = ql.reshape(B, SWA_KV_HEADS, SWA_GROUP, nb, Bk, HEAD_DIM)

    def band(a):
        ap = jnp.pad(a, ((0, 0), (0, 0), (Bk, Bk), (0, 0))).reshape(B, SWA_KV_HEADS, nb + 2, Bk, HEAD_DIM)
        return jnp.concatenate([ap[:, :, :-2], ap[:, :, 1:-1], ap[:, :, 2:]], axis=3)

    kb, vb = band(kl), band(vl)
    blk = jnp.arange(nb)[:, None]
    qpos = blk * Bk + jnp.arange(Bk)[None, :]
    kpos = (blk - 1) * Bk + jnp.arange(3 * Bk)[None, :]
    valid = ((jnp.abs(qpos[:, :, None] - kpos[:, None, :]) <= WINDOW)
             & (kpos[:, None, :] >= 0) & (kpos[:, None, :] < T))
    s_loc = jnp.where(valid, jnp.einsum('bhgnqd,bhnkd->bhgnqk', qb, kb), -jnp.inf)
    s_ctx = jnp.einsum('bhgnqd,bhld->bhgnql', qb, kc)
    s_sink = jnp.broadcast_to(sink[None, :, :, None, None, None], s_loc.shape[:-1] + (1,))
    p = jax.nn.softmax(jnp.concatenate([s_loc, s_ctx, s_sink], axis=-1), axis=-1)
    o = (jnp.einsum('bhgnqk,bhnkd->bhgnqd', p[..., :3 * Bk], vb)
         + jnp.einsum('bhgnql,bhld->bhgnqd', p[..., 3 * Bk:3 * Bk + L], vc))
    y_l = from_heads(o.reshape(B, SWA_HEADS, T, HEAD_DIM)).astype(q_l.dtype)
    y_c = None
    if need_ctx:
        qc = (to_heads(q_c, SWA_HEADS).astype(F32) * scale).reshape(B, SWA_KV_HEADS, SWA_GROUP, L, HEAD_DIM)
        s_cc = jnp.einsum('bhgld,bhmd->bhglm', qc, kc)
        s_sk = jnp.broadcast_to(sink[None, :, :, None, None], s_cc.shape[:-1] + (1,))
        pc = jax.nn.softmax(jnp.concatenate([s_cc, s_sk], axis=-1), axis=-1)
        oc = jnp.einsum('bhglm,bhmd->bhgld', pc[..., :L], vc)
        y_c = from_heads(oc.reshape(B, SWA_HEADS, L, HEAD_DIM)).astype(q_c.dtype)
    return y_l, y_c


def hgrn2_group(p_lat, p_ctx, lb, norm_g, need_ctx):
    lbh = lb.astype(F32).reshape(2, HG_HEADS, HEAD_DIM)

    def prep(q, f, i):
        B, T, _ = f.shape
        z = f.reshape(B, T, 2, HG_HEADS, HEAD_DIM).astype(F32)
        logf = jnp.logaddexp(jnp.log(lbh), jnp.log1p(-lbh) + jax.nn.log_sigmoid(z))
        k = (1 - lbh) * jax.nn.sigmoid(-z)
        perm = (2, 0, 3, 1, 4)
        return (to_heads(q, HG_HEADS).astype(F32), to_heads(i, HG_HEADS).astype(F32),
                k.transpose(perm), logf.transpose(perm))

    q_l, f_l, i_l, gate_l = p_lat
    q_c, f_c, i_c, gate_c = p_ctx
    ql, vl, kl, lfl = prep(q_l, f_l, i_l)
    qc, vc, kc, lfc = prep(q_c, f_c, i_c)
    B = ql.shape[0]
    o_l = jnp.zeros_like(vl)
    o_c = jnp.zeros_like(vc)
    for d in range(2):
        S0 = jnp.zeros((B, HG_HEADS, HEAD_DIM, HEAD_DIM), F32)
        oc, S_ctx = gla_scan(flip_t(qc, d), flip_t(kc[d], d), flip_t(vc, d), flip_t(lfc[d], d), S0)
        ol, _ = gla_scan(flip_t(ql, d), flip_t(kl[d], d), flip_t(vl, d), flip_t(lfl[d], d), S_ctx)
        o_l = o_l + flip_t(ol, d)
        o_c = o_c + flip_t(oc, d)
    y_l = head_norm_gate(o_l, norm_g, gate_l)
    y_c = head_norm_gate(o_c, norm_g, gate_c) if need_ctx else None
    return y_l, y_c


def mixer_layer(hl, hc, w_in, w_out, dn_conv, dn_A_log, dn_dt_bias, dn_norm, swa_sink,
                hg_lb, hg_norm, cos, sin, need_ctx):
    pl = split_in(hl @ w_in)
    pc = split_in(hc @ w_in)
    dn_l, dn_c = deltanet_group(pl[0:4], pc[0:4], dn_conv, dn_A_log, dn_dt_bias, dn_norm, need_ctx)
    sw_l, sw_c = swa_group(pl[4], pl[5], pl[6], pc[4], pc[5], pc[6], swa_sink, cos, sin, need_ctx)
    hg_l, hg_c = hgrn2_group(pl[7:11], pc[7:11], hg_lb, hg_norm, need_ctx)
    yl = jnp.concatenate([dn_l, sw_l, hg_l], axis=-1) @ w_out
    yc = jnp.concatenate([dn_c, sw_c, hg_c], axis=-1) @ w_out if need_ctx else None
    return yl, yc


def sqrelu_mlp(h, w1, w2):
    return jnp.square(jax.nn.relu(h @ w1)) @ w2


def setup_inputs(seed: int = 0) -> dict:
    key = jax.random.key(seed)
    ks = jax.random.split(key, 20)

    def nrm(k, shape, s):
        return jax.random.normal(k, shape, F32) * s

    dt = jnp.exp(jax.random.uniform(ks[11], (DEPTH, 2, DN_HEADS), F32, math.log(1e-3), math.log(1e-1)))
    return {
        'x': nrm(ks[0], (BATCH, SEQ, D_MODEL), 1.0),
        'c': nrm(ks[1], (BATCH, D_MODEL), 1.0),
        'ctx': nrm(ks[2], (BATCH, CTX_LEN, D_MODEL), 1.0),
        'c_ctx': nrm(ks[3], (D_MODEL,), 1.0),
        'w_ada': nrm(ks[4], (DEPTH, D_MODEL, N_MOD * D_MODEL), 0.5 * D_MODEL ** -0.5),
        'b_ada': nrm(ks[5], (DEPTH, N_MOD * D_MODEL), 0.01),
        'norm1': 1.0 + nrm(ks[6], (DEPTH, D_MODEL), 0.02),
        'norm2': 1.0 + nrm(ks[7], (DEPTH, D_MODEL), 0.02),
        'w_in': nrm(ks[8], (DEPTH, D_MODEL, D_IN), D_MODEL ** -0.5),
        'dn_conv': nrm(ks[9], (DEPTH, CONV_W, 3 * DN_WIDTH), CONV_W ** -0.5),
        'dn_A_log': jnp.log(jax.random.uniform(ks[10], (DEPTH, 2, DN_HEADS), F32, 1.0, 16.0)),
        'dn_dt_bias': dt + jnp.log(-jnp.expm1(-dt)),
        'dn_norm': 1.0 + nrm(ks[12], (DEPTH, HEAD_DIM), 0.02),
        'swa_sink': nrm(ks[13], (DEPTH, SWA_HEADS), 0.5),
        'hg_lb_logits': nrm(ks[14], (2, DEPTH, HG_WIDTH), 0.5),
        'hg_norm': 1.0 + nrm(ks[15], (DEPTH, HEAD_DIM), 0.02),
        'w_out': nrm(ks[16], (DEPTH, D_MIX, D_MODEL), D_MIX ** -0.5),
        'w_ff1': nrm(ks[17], (DEPTH, D_MODEL, D_FF), D_MODEL ** -0.5),
        'w_ff2': nrm(ks[18], (DEPTH, D_FF, D_MODEL), D_FF ** -0.5),
        'norm_f': 1.0 + nrm(ks[19], (D_MODEL,), 0.02),
    }


def reference(x, c, ctx, c_ctx, w_ada, b_ada, norm1, norm2, w_in, dn_conv, dn_A_log, dn_dt_bias,
              dn_norm, swa_sink, hg_lb_logits, hg_norm, w_out, w_ff1, w_ff2, norm_f):
    B, T, _ = x.shape
    cos, sin = axial_rope_tables(T)
    lb_all = jnp.cumsum(jax.nn.softmax(hg_lb_logits.astype(F32), axis=1), axis=1)
    lb_all = lb_all - lb_all[:, :1]
    xl, xc = x, ctx
    for l in range(DEPTH):
        need_ctx = l < DEPTH - 1
        ml = (jax.nn.silu(c) @ w_ada[l] + b_ada[l]).reshape(B, N_MOD, 1, D_MODEL)
        mc = (jax.nn.silu(c_ctx) @ w_ada[l] + b_ada[l]).reshape(1, N_MOD, 1, D_MODEL)
        hl = modulate(rmsnorm(xl, norm1[l]), ml[:, 0], ml[:, 1])
        hc = modulate(rmsnorm(xc, norm1[l]), mc[:, 0], mc[:, 1])
        yl, yc = mixer_layer(hl, hc, w_in[l], w_out[l], dn_conv[l], dn_A_log[l], dn_dt_bias[l],
                             dn_norm[l], swa_sink[l], lb_all[:, l], hg_norm[l], cos, sin, need_ctx)
        xl = xl + ml[:, 2] * yl
        xl = xl + ml[:, 5] * sqrelu_mlp(modulate(rmsnorm(xl, norm2[l]), ml[:, 3], ml[:, 4]), w_ff1[l], w_ff2[l])
        if need_ctx:
            xc = xc + mc[:, 2] * yc
            xc = xc + mc[:, 5] * sqrelu_mlp(modulate(rmsnorm(xc, norm2[l]), mc[:, 3], mc[:, 4]), w_ff1[l], w_ff2[l])
    return rmsnorm(xl, norm_f)
```

```python
import contextlib
import numpy as np
import ml_dtypes
import concourse.bass as bass
import concourse.mybir as mybir
from concourse.bass_utils import run_bass_kernel_spmd

F32 = mybir.dt.float32
F32R = mybir.dt.float32r
BF16 = mybir.dt.bfloat16
AF = mybir.ActivationFunctionType
ALU = mybir.AluOpType
AX = mybir.AxisListType

SAME_ENGINE_SYNC = True

D = 1024
T_LAT = 2048
T_CTX = 256
TT = T_LAT + T_CTX
NT = TT // 128
DEPTH = 4
EPS = 1e-6
NEG = -30000.0


class Res:
    __slots__ = ("name", "t", "w", "r")

    def __init__(self, name, t):
        self.name = name
        self.t = t
        self.w = {}
        self.r = {}

    def __getitem__(self, k):
        return self.t[k]


class Prog:
    ENGS = ("pe", "act", "dve", "pool", "sp")

    def __init__(self, nc, n_dma_sems=16):
        self.nc = nc
        self.ops = {e: [] for e in self.ENGS}
        self.cnt = {e: 0 for e in self.ENGS}
        self.sem = {e: nc.alloc_semaphore("c_" + e) for e in self.ENGS}
        self.dsem = [nc.alloc_semaphore("d%d" % i) for i in range(n_dma_sems)]
        self.dval = [0] * n_dma_sems
        half = n_dma_sems // 2
        self.dq = {"sp": list(range(0, half)), "pool": list(range(half, n_dma_sems)), "act": []}
        self.dnext = {"sp": 0, "pool": 0, "act": 0}
        self.waited = {}
        self.n_inst = 0
        self.scopes = []
        self.banks = []
        self.bank_i = 0
        self.uid = 0

    def sb(self, name, shape, dtype):
        self.uid += 1
        nm = "%s_%d" % (name, self.uid)
        if self.scopes:
            t = self.scopes[-1].enter_context(self.nc.sbuf_tensor(nm, list(shape), dtype))
        else:
            t = self.nc.alloc_sbuf_tensor(nm, list(shape), dtype)
        return Res(nm, t)

    @contextlib.contextmanager
    def scope(self):
        st = contextlib.ExitStack()
        self.scopes.append(st)
        try:
            yield
            self.barrier()
        finally:
            self.scopes.pop()
            st.close()

    def dram(self, name, shape, dtype, kind="Internal"):
        return Res(name, self.nc.dram_tensor(name, list(shape), dtype, kind=kind))

    def init_banks(self, n=8):
        for i in range(n):
            self.banks.append(Res("bank%d" % i, self.nc.alloc_psum_tensor("bank%d" % i, [128, 512], F32)))

    def bank(self):
        b = self.banks[self.bank_i]
        self.bank_i = (self.bank_i + 1) % len(self.banks)
        return b

    @staticmethod
    def _norm(x):
        return (x, None) if isinstance(x, Res) else x

    def _collect(self, reads, writes):
        evs = {}

        def add(k, v):
            if evs.get(k, -1) < v:
                evs[k] = v

        for x in reads:
            res, sub = self._norm(x)
            subs = list(res.w.keys()) if sub is None else (sub, None)
            for s in subs:
                ev = res.w.get(s)
                if ev is not None:
                    add(*ev)
        for x in writes:
            res, sub = self._norm(x)
            subs = (set(res.w.keys()) | set(res.r.keys())) if sub is None else (sub, None)
            for s in subs:
                ev = res.w.get(s)
                if ev is not None:
                    add(*ev)
                rr = res.r.get(s)
                if rr:
                    for k, v in rr.items():
                        add(k, v)
        return evs

    def _record(self, reads, writes, ev):
        k, v = ev
        for x in reads:
            res, sub = self._norm(x)
            d = res.r.setdefault(sub, {})
            if d.get(k, -1) < v:
                d[k] = v
        for x in writes:
            res, sub = self._norm(x)
            if sub is None:
                res.w = {None: ev}
                res.r = {}
            else:
                res.w[sub] = ev
                res.r[sub] = {}

    def _waits(self, eng, evs):
        out = []
        for k, v in evs.items():
            if k == ("c", eng) and (eng in ("pe", "sp") or not SAME_ENGINE_SYNC):
                continue
            if self.waited.get((eng, k), -1) >= v:
                continue
            self.waited[(eng, k)] = v
            out.append((k, v))
        return out

    def _semof(self, k):
        return self.sem[k[1]] if k[0] == "c" else self.dsem[k[1]]

    def op(self, eng, fn, reads=(), writes=()):
        evs = self._collect(reads, writes)
        waits = self._waits(eng, evs)
        self.cnt[eng] += 1
        self._record(reads, writes, (("c", eng), self.cnt[eng]))
        self.ops[eng].append((waits, fn, self.sem[eng], 1))
        self.n_inst += 1

    def dma(self, out_ap, in_ap, reads=(), writes=(), queue="sp", **kw):
        evs = self._collect(reads, writes)
        lst = self.dq[queue]
        i = lst[self.dnext[queue] % len(lst)]
        self.dnext[queue] += 1
        if self.dval[i] > 0:
            k = ("d", i)
            if evs.get(k, -1) < self.dval[i]:
                evs[k] = self.dval[i]
        waits = self._waits(queue, evs)
        self.dval[i] += 16
        self._record(reads, writes, (("d", i), self.dval[i]))

        def fn(e, out_ap=out_ap, in_ap=in_ap, kw=kw):
            return e.dma_start(out=out_ap, in_=in_ap, **kw)
        self.ops[queue].append((waits, fn, self.dsem[i], 16))
        self.n_inst += 1

    def _all_events(self):
        evs = {("c", e): self.cnt[e] for e in self.ENGS if self.cnt[e] > 0}
        for i, v in enumerate(self.dval):
            if v > 0:
                evs[("d", i)] = v
        return evs

    def barrier(self):
        evs = self._all_events()
        for e in self.ENGS:
            w = []
            for k, v in evs.items():
                if k == ("c", e):
                    continue
                if self.waited.get((e, k), -1) >= v:
                    continue
                self.waited[(e, k)] = v
                w.append((k, v))
            if w:
                self.ops[e].append((w, None, None, 0))

    def final_wait(self, eng="sp"):
        evs = self._all_events()
        w = [(k, v) for k, v in evs.items() if k != ("c", eng)]
        self.ops[eng].append((w, None, None, 0))

    def emit(self):
        hmap = {"pe": "tensor", "act": "scalar", "dve": "vector", "pool": "gpsimd", "sp": "sync"}
        with self.nc.Block() as block:
            for e in self.ENGS:
                lst = self.ops[e]
                if not lst:
                    continue

                def body(h, lst=lst):
                    for waits, fn, sem, inc in lst:
                        for k, v in waits:
                            h.wait_ge(self._semof(k), v)
                        if fn is not None:
                            fn(h).then_inc(sem, inc)
                getattr(block, hmap[e])(body)

    def mm(self, out_ap, lhsT, rhs, start, stop, reads, writes):
        self.op("pe", lambda e: e.matmul(out_ap, lhsT=lhsT, rhs=rhs, start=start, stop=stop), reads, writes)

    def tr(self, out_ap, in_ap, ident_ap, reads, writes):
        self.op("pe", lambda e: e.transpose(out_ap, in_ap, ident_ap), reads, writes)

    def act(self, out_ap, in_ap, func, reads, writes, scale=None, bias=None):
        kw = {}
        if scale is not None:
            kw["scale"] = scale
        if bias is not None:
            kw["bias"] = bias
        self.op("act", lambda e: e.activation(out=out_ap, in_=in_ap, func=func, **kw), reads, writes)

    def tt(self, eng, out_ap, in0, in1, op, reads, writes):
        self.op(eng, lambda e: e.tensor_tensor(out=out_ap, in0=in0, in1=in1, op=op), reads, writes)

    def ts(self, eng, out_ap, in0, s1, s2, op0, op1, reads, writes):
        if op1 is None:
            self.op(eng, lambda e: e.tensor_scalar(out=out_ap, in0=in0, scalar1=s1, scalar2=None, op0=op0), reads, writes)
        else:
            self.op(eng, lambda e: e.tensor_scalar(out=out_ap, in0=in0, scalar1=s1, scalar2=s2, op0=op0, op1=op1), reads, writes)

    def stt(self, out_ap, in0, scalar, in1, op0, op1, reads, writes):
        self.op("dve", lambda e: e.scalar_tensor_tensor(out=out_ap, in0=in0, scalar=scalar, in1=in1, op0=op0, op1=op1),
                reads, writes)

    def cp(self, eng, out_ap, in_ap, reads, writes):
        if eng == "act":
            self.op("act", lambda e: e.activation(out=out_ap, in_=in_ap, func=AF.Copy), reads, writes)
        else:
            self.op(eng, lambda e: e.tensor_copy(out=out_ap, in_=in_ap), reads, writes)

    def memset(self, eng, ap, val, writes):
        self.op(eng, lambda e: e.memset(ap, val), (), writes)

    def red(self, out_ap, in_ap, op, reads, writes):
        self.op("dve", lambda e: e.tensor_reduce(out=out_ap, in_=in_ap, axis=AX.X, op=op), reads, writes)

    def recip(self, out_ap, in_ap, reads, writes):
        self.op("dve", lambda e: e.reciprocal(out=out_ap, in_=in_ap), reads, writes)


def _consts():
    c = {}
    i = np.arange(128)
    c["ident"] = np.eye(128, dtype=np.float32)
    perm = np.zeros((128, 128), np.float32)
    for m in range(128):
        partner = m + 32 if (m % 64) < 32 else m - 32
        perm[partner, m] = 1.0
    c["perm"] = perm
    blk = np.zeros((128, 128), np.float32)
    blk[:64, :64] = 1.0 / 64
    blk[64:, 64:] = 1.0 / 64
    c["blk64"] = blk
    t = np.arange(T_LAT)
    row = (t // 64).astype(np.float32)
    col = (t % 64).astype(np.float32)
    half = 32
    inv = (10000.0 ** (-np.arange(0, half, 2, dtype=np.float32) / half)).astype(np.float32)
    ang = np.concatenate([row[:, None] * inv, col[:, None] * inv], axis=-1).astype(np.float32)
    cos = np.cos(ang).astype(np.float32)
    sin = np.sin(ang).astype(np.float32)
    cosT = np.zeros((128, T_LAT), np.float32)
    sinsT = np.zeros((128, T_LAT), np.float32)
    for p in range(128):
        dd = p % 64
        cosT[p] = cos[:, dd % 32]
        sinsT[p] = sin[:, dd % 32] * (-1.0 if dd < 32 else 1.0)
    c["cosT"] = cosT
    c["sinsT"] = sinsT
    c["swa_prev"] = (i[:, None] >= i[None, :]).astype(np.float32)
    c["swa_next"] = (i[:, None] <= i[None, :]).astype(np.float32)
    dn = np.zeros((2, 3 + 7, 128, 128), np.float32)
    for d in range(2):
        if d == 0:
            tri = (i[:, None] <= i[None, :])
            sm = (i[:, None] > i[None, :])
            att = (i[None, :] >= i[:, None])
        else:
            tri = (i[:, None] >= i[None, :])
            sm = (i[:, None] < i[None, :])
            att = (i[None, :] <= i[:, None])
        dn[d, 0] = tri
        dn[d, 1] = sm
        dn[d, 2] = np.where(att, 0.0, NEG)
        for li in range(7):
            s = 1 << li
            same2 = (i[:, None] // (2 * s)) == (i[None, :] // (2 * s))
            diff1 = (i[:, None] // s) != (i[None, :] // s)
            strict = att & (i[:, None] != i[None, :])
            dn[d, 3 + li] = (same2 & diff1 & strict)
    c["dn_masks"] = dn
    CH = 32
    l64 = np.arange(CH)
    hg_a = np.zeros((2, CH, 2 * CH), np.float32)
    hg_last = np.zeros((2, CH, CH), np.float32)
    hg_mask = np.zeros((2, CH, CH), np.float32)
    for d in range(2):
        if d == 0:
            tri = (l64[:, None] <= l64[None, :]).astype(np.float32)
            mid = CH // 2 - 1
        else:
            tri = (l64[:, None] >= l64[None, :]).astype(np.float32)
            mid = CH // 2
        hg_a[d, :, 0:CH] = tri - tri[:, mid:mid + 1]
        hg_a[d, :, CH:2 * CH] = tri
        hg_last[d] = 1.0 - tri
        hg_mask[d] = (l64[:, None] <= l64[None, :]) if d == 0 else (l64[:, None] >= l64[None, :])
    c["hg_a"] = hg_a
    c["hg_last"] = hg_last
    c["hg_mask"] = hg_mask
    c["ones64"] = np.full((64, 64), 1.0 / 64, np.float32)
    return c


IN_OFF = dict(dn_qkv=0, dn_gate=768, dn_a=1024, dn_b=1032, sw_q=1040, sw_k=1552, sw_v=1680,
              hg_q=1808, hg_f=2064, hg_i=2576, hg_gate=2832)


def _prep_shared(inp):
    f = lambda a: np.ascontiguousarray(a, dtype=np.float32)
    s = {}
    w_in = inp["w_in"]
    o = IN_OFF
    s["w_dn"] = f(np.concatenate([w_in[:, :, 0:768], w_in[:, :, 768:1024], w_in[:, :, 1024:1040]], axis=2))
    s["w_sw"] = f(w_in[:, :, o["sw_q"]:o["sw_q"] + 768])
    s["w_hg"] = f(w_in[:, :, o["hg_q"]:o["hg_q"] + 1280])
    s["w_ada"] = f(inp["w_ada"])
    s["w_out"] = f(inp["w_out"])
    s["w_ff1"] = f(inp["w_ff1"])
    s["w_ff2"] = f(inp["w_ff2"])
    fm = lambda v: f(v.reshape(-1, 128).T)
    s["b_ada"] = f(np.stack([fm(inp["b_ada"][l]) for l in range(DEPTH)], axis=1))
    s["norm1"] = f(np.stack([fm(inp["norm1"][l]) for l in range(DEPTH)], axis=1))
    s["norm2"] = f(np.stack([fm(inp["norm2"][l]) for l in range(DEPTH)], axis=1))
    s["norm_f"] = fm(inp["norm_f"])
    s["dn_conv"] = f(np.transpose(inp["dn_conv"].reshape(DEPTH, 5, 6, 128), (3, 0, 2, 1)))
    rep = lambda v: f(np.broadcast_to(v.reshape(1, -1), (128, v.size)))
    s["dn_alog"] = rep(inp["dn_A_log"])
    s["dn_dtb"] = rep(inp["dn_dt_bias"])
    s["sink"] = rep(inp["swa_sink"])
    tile2 = lambda v: f(np.concatenate([v, v], axis=1).T)
    s["dn_norm"] = tile2(inp["dn_norm"])
    s["hg_norm"] = tile2(inp["hg_norm"])
    s["lb_rep"] = rep(inp["hg_lb_logits"])
    lbf = inp["hg_lb_logits"].reshape(2, DEPTH, 4, 64)
    s["lb_fm"] = f(np.transpose(lbf, (3, 0, 1, 2)))
    s.update(_consts())
    return s


ECLAMP = 2.35e17


def r_(ap):
    return ap.bitcast(F32R)


def build_program(n_layers=DEPTH, dbg=False, groups=("swa", "hg", "dn"), WL=DEPTH, stop=None):
    nc = bass.Bass("TRN2", target_bir_lowering=False)
    P = Prog(nc)
    P.init_banks(8)
    L = n_layers
    last_global = (n_layers == DEPTH)

    def din(name, shape):
        return P.dram(name, shape, F32, kind="ExternalInput")

    x_in = din("x", [2, T_LAT, D])
    ctx_in = din("ctx", [2, T_CTX, D])
    cT_in = din("cT", [128, 8, 3])
    w_ada = din("w_ada", [WL, D, 6 * D])
    b_ada = din("b_ada", [128, DEPTH, 48])
    norm1_in = din("norm1", [128, DEPTH, 8])
    norm2_in = din("norm2", [128, DEPTH, 8])
    normf_in = din("norm_f", [128, 8])
    w_dn = din("w_dn", [WL, D, 1040])
    w_sw = din("w_sw", [WL, D, 768])
    w_hg = din("w_hg", [WL, D, 1280])
    w_out = din("w_out", [WL, D, D])
    w_ff1 = din("w_ff1", [WL, D, 4 * D])
    w_ff2 = din("w_ff2", [WL, 4 * D, D])
    dn_conv_in = din("dn_conv", [128, DEPTH, 6, 5])
    dn_alog_in = din("dn_alog", [128, DEPTH * 8])
    dn_dtb_in = din("dn_dtb", [128, DEPTH * 8])
    sink_in = din("sink", [128, DEPTH * 8])
    dn_norm_in = din("dn_norm", [128, DEPTH])
    hg_norm_in = din("hg_norm", [128, DEPTH])
    lb_rep_in = din("lb_rep", [128, 2 * DEPTH * 256])
    lb_fm_in = din("lb_fm", [64, 2, DEPTH, 4])
    c_ident = din("ident", [128, 128])
    c_perm = din("perm", [128, 128])
    c_blk64 = din("blk64", [128, 128])
    c_cosT = din("cosT", [128, T_LAT])
    c_sinsT = din("sinsT", [128, T_LAT])
    c_swa_prev = din("swa_prev", [128, 128])
    c_swa_next = din("swa_next", [128, 128])
    c_dn_masks = din("dn_masks", [2, 10, 128, 128])
    c_hg_a = din("hg_a", [2, 32, 64])
    c_hg_last = din("hg_last", [2, 32, 32])
    c_hg_mask = din("hg_mask", [2, 32, 32])
    c_ones64 = din("ones64", [64, 64])

    out_d = P.dram("out", [2, T_LAT, D], F32, kind="ExternalOutput")
    XT = P.dram("XT", [2, 128, 8, TT], F32)
    QKV = P.dram("QKVs", [TT, 768], F32)
    LBS = P.dram("LBS", [2, 128, 2, DEPTH, 256], F32)
    HGS = P.dram("HGS", [TT, 2, 3, 256], F32)
    OS = P.dram("OS", [2, 64, 4, TT], F32)
    dbg_out = {}

    def dbg_tensor(name, shape):
        r = P.dram("dbg_" + name, shape, F32, kind="ExternalOutput")
        dbg_out[name] = r
        return r

    def xk(b, c0, n):
        return [(XT, (b, ti)) for ti in range(c0 // 128, (c0 + n + 127) // 128)]

    ident = P.sb("ident", [128, 128], F32)
    ones_bf = P.sb("ones_bf", [128, 128], BF16)
    ones_f = P.sb("ones_f", [128, 128], F32)
    blk64 = P.sb("blk64", [128, 128], BF16)
    cT = P.sb("cT", [128, 8, 3], F32)
    scT = P.sb("scT", [128, 8, 3], F32)
    modv = P.sb("modv", [128, 6, 8, 3], F32)
    g1 = P.sb("g1", [128, 8, 3], F32)
    g2 = P.sb("g2", [128, 8, 3], F32)
    bada = P.sb("bada", [128, DEPTH, 48], F32)
    n1 = P.sb("n1", [128, DEPTH, 8], F32)
    n2 = P.sb("n2", [128, DEPTH, 8], F32)
    nf = P.sb("nf", [128, 8], F32)
    dnnorm = P.sb("dnnorm", [128, DEPTH], F32)
    hgnorm = P.sb("hgnorm", [128, DEPTH], F32)
    epsb = P.sb("epsb", [128, 1], F32)
    oneb = P.sb("oneb", [128, 1], F32)
    zero_f = P.sb("zero_f", [128, 128], F32)
    lbfm = P.sb("lbfm", [64, 2, DEPTH, 4], F32)
    oml_fm = P.sb("oml_fm", [64, 2, DEPTH, 4], F32)
    ones64 = P.sb("ones64", [64, 64], BF16)
    zero256 = P.sb("zero256", [128, 256], F32)
    hT = None
    yT = None

    for dst, src in ((ident, c_ident), (cT, cT_in), (bada, b_ada), (n1, norm1_in), (n2, norm2_in), (nf, normf_in),
                     (dnnorm, dn_norm_in), (hgnorm, hg_norm_in)):
        P.dma(dst[:], src[:], reads=[src], writes=[dst])
    P.memset("pool", ones_bf[:], 1.0, [ones_bf])
    P.memset("pool", ones_f[:], 1.0, [ones_f])
    P.memset("pool", epsb[:], EPS, [epsb])
    P.memset("pool", oneb[:], 1.0, [oneb])
    P.memset("pool", zero_f[:], 0.0, [zero_f])
    P.memset("pool", zero256[:], 0.0, [zero256])
    P.dma(ones64[:], c_ones64[:], reads=[c_ones64], writes=[ones64], queue="pool")
    P.dma(blk64[:], c_blk64[:], reads=[c_blk64], writes=[blk64], queue="pool")
    P.act(scT[:], cT[:], AF.Silu, [cT], [scT])

    def rstd_from_sumsq(ps_ap, out_ap, scale, ps_res, out_res, npart=128):
        P.act(out_ap, ps_ap, AF.Ln, [ps_res, epsb], [out_res], scale=scale, bias=epsb[0:npart, 0:1])
        P.act(out_ap, out_ap, AF.Exp, [out_res], [out_res], scale=-0.5)

    def blocks512():
        yield (0, 256, 2)
        for k in range(4):
            yield (256 + 512 * k, 512, None)

    def load_w(dst, src_ap, n_k, src_res):
        for kc in range(n_k):
            P.dma(dst[:, kc, :], src_ap[kc * 128:(kc + 1) * 128, :], reads=[src_res], writes=[(dst, kc)], queue="pool")

    def feat_proj(wg, col0, c0, n):
        ps = P.bank()
        for kc in range(8):
            P.mm(ps[:, :n], wg[:, kc, col0:col0 + 128], hT[:, kc, c0:c0 + n], kc == 0, kc == 7, [wg, hT], [ps])
        return ps

    def tok_proj(ps_ap, ps, wg, col0, ncol, ti):
        for kc in range(8):
            P.mm(ps_ap, hT[:, kc, ti * 128:(ti + 1) * 128], wg[:, kc, col0:col0 + ncol], kc == 0, kc == 7, [wg, hT], [ps])

    def chain_steps(stepfn, d, n):
        for step in range(n):
            yield from stepfn(d, step)
            yield

    def run_streams(streams, lag):
        live = list(streams)
        tick = 0
        started = 1
        while live:
            for i, g in enumerate(list(live[:started])):
                try:
                    next(g)
                except StopIteration:
                    live.remove(g)
                    started -= 1
            tick += 1
            if started < len(live) and tick >= lag * started:
                started += 1

    def scan_order(d):
        return list(range(NT)) if d == 0 else [1, 0] + list(range(NT - 1, 1, -1))

    def head_out(l, need_ctx, wg, gcol0, gain, ych0):
        sqs = [P.sb("hn_sq", [64, 512], BF16) for _ in range(2)]
        rss = [P.sb("hn_rs", [64, 512], F32) for _ in range(2)]
        sgs = [P.sb("hn_sg", [64, 512], BF16) for _ in range(2)]
        yts = [P.sb("hn_yt", [64, 512], BF16) for _ in range(2)]
        o0s = [P.sb("hn_o0", [64, 512], F32) for _ in range(2)]
        o1s = [P.sb("hn_o1", [64, 512], F32) for _ in range(2)]
        k = 0
        for (c0, n, r) in blocks512():
            if r == 2 and not need_ctx:
                continue
            for h in range(4):
                sq, rs, sgt, yt, o0, o1 = sqs[k % 2], rss[k % 2], sgs[k % 2], yts[k % 2], o0s[k % 2], o1s[k % 2]
                k += 1
                okeys = lambda dd: [(OS, (dd, t)) for t in range(c0 // 128, (c0 + n) // 128)]
                P.dma(o0[:, :n], OS[0, :, h, c0:c0 + n], reads=okeys(0), writes=[o0])
                P.dma(o1[:, :n], OS[1, :, h, c0:c0 + n], reads=okeys(1), writes=[o1])
                P.tt("pool", o0[:, :n], o0[:, :n], o1[:, :n], ALU.add, [o0, o1], [o0])
                P.act(sq[:, :n], o0[:, :n], AF.Square, [o0], [sq])
                ps = P.bank()
                P.mm(ps[0:64, :n], ones64[:], sq[:, :n], True, True, [ones64, sq], [ps])
                rstd_from_sumsq(ps[0:64, :n], rs[:, :n], 1.0, ps, rs, 64)
                ps2 = P.bank()
                for kc in range(8):
                    P.mm(ps2[0:64, :n], wg[:, kc, gcol0 + h * 64:gcol0 + (h + 1) * 64], hT[:, kc, c0:c0 + n], kc == 0, kc == 7,
                         [wg, hT], [ps2])
                P.act(sgt[:, :n], ps2[0:64, :n], AF.Silu, [ps2], [sgt])
                P.tt("dve", rs[:, :n], o0[:, :n], rs[:, :n], ALU.mult, [o0, rs], [rs])
                if h % 2 == 0:
                    P.stt(yT[0:64, ych0 + h // 2, c0:c0 + n], rs[:, :n], gain[0:64, l:l + 1], sgt[:, :n], ALU.mult, ALU.mult,
                          [rs, gain, sgt], [yT])
                else:
                    P.stt(yt[:, :n], rs[:, :n], gain[0:64, l:l + 1], sgt[:, :n], ALU.mult, ALU.mult, [rs, gain, sgt], [yt])
                    P.dma(yT[64:128, ych0 + h // 2, c0:c0 + n], yt[:, :n], reads=[yt], writes=[yT])

    def swa_group(l, b, need_ctx):
        with P.scope():
            wg = P.sb("wsw", [128, 8, 768], BF16)
            load_w(wg, w_sw[l], 8, w_sw)
            qT = P.sb("sqT", [64, 8, TT], BF16)
            kT = P.sb("skT", [64, 2, TT], BF16)
            vaug = P.sb("vaug", [128, NT, 2, 72], BF16)
            cosT = P.sb("cosT", [64, T_LAT], F32)
            sinsT = P.sb("sinsT", [64, T_LAT], F32)
            perm = P.sb("perm", [64, 64], F32)
            mprev = P.sb("mprev", [128, 128], BF16)
            mnext = P.sb("mnext", [128, 128], BF16)
            esink = P.sb("esink", [128, 8], F32)
            P.dma(cosT[:], c_cosT[0:64, :], reads=[c_cosT], writes=[cosT])
            P.dma(sinsT[:], c_sinsT[0:64, :], reads=[c_sinsT], writes=[sinsT])
            P.dma(perm[:], c_perm[0:64, 0:64], reads=[c_perm], writes=[perm])
            P.dma(mprev[:], c_swa_prev[:], reads=[c_swa_prev], writes=[mprev], queue="pool")
            P.dma(mnext[:], c_swa_next[:], reads=[c_swa_next], writes=[mnext], queue="pool")
            P.dma(esink[:], sink_in[:, l * 8:(l + 1) * 8], reads=[sink_in], writes=[esink])
            P.act(esink[:], esink[:], AF.Exp, [esink], [esink])
            P.memset("pool", vaug[:].rearrange("p a b c -> p (a b c)"), 1.0, [vaug])
            qraw = [P.sb("qraw", [64, 512], F32) for _ in range(2)]
            t1s = [P.sb("rt1", [64, 512], F32) for _ in range(2)]
            t2s = [P.sb("rt2", [64, 512], F32) for _ in range(2)]
            k = 0
            for (c0, n, r) in blocks512():
                for hh in range(10):
                    ps = P.bank()
                    for kc in range(8):
                        P.mm(ps[0:64, :n], wg[:, kc, hh * 64:(hh + 1) * 64], hT[:, kc, c0:c0 + n], kc == 0, kc == 7, [wg, hT], [ps])
                    dst = qT[:, hh, c0:c0 + n] if hh < 8 else kT[:, hh - 8, c0:c0 + n]
                    dres = qT if hh < 8 else kT
                    if c0 < T_CTX:
                        P.cp("act", dst, ps[0:64, :n], [ps], [dres])
                    else:
                        tl = c0 - T_CTX
                        qr, t1, t2 = qraw[k % 2], t1s[k % 2], t2s[k % 2]
                        k += 1
                        P.cp("act", qr[:, :n], ps[0:64, :n], [ps], [qr])
                        ps2 = P.bank()
                        P.mm(ps2[0:64, :n], perm[:], qr[:, :n], True, True, [perm, qr], [ps2])
                        P.tt("pool", t1[:, :n], qr[:, :n], cosT[:, tl:tl + n], ALU.mult, [qr, cosT], [t1])
                        P.tt("dve", t2[:, :n], ps2[0:64, :n], sinsT[:, tl:tl + n], ALU.mult, [ps2, sinsT], [t2])
                        P.tt("dve", dst, t1[:, :n], t2[:, :n], ALU.add, [t1, t2], [dres])
            for ti in range(NT):
                ps = P.bank()
                tok_proj(ps[:, 0:128], ps, wg, 640, 128, ti)
                P.cp("act", vaug[:, ti, :, 0:64], ps[:, 0:128].rearrange("p (g d) -> p g d", g=2), [ps], [vaug])
            if stop == "swa_proj":
                return
            Es = [P.sb("E", [128, 512], BF16) for _ in range(10)]
            dens = [P.sb("den", [128, 4], F32) for _ in range(2)]
            ytoks = [P.sb("ytok", [128, 256], F32) for _ in range(2)]
            ei = 0
            gi = 0
            qtiles = ([0, 1] if need_ctx else []) + list(range(2, NT))
            for ti in qtiles:
                if ti < 2:
                    kblocks = [(0, None), (1, None)]
                else:
                    kblocks = []
                    if ti - 1 >= 2:
                        kblocks.append((ti - 1, mprev))
                    kblocks.append((ti, None))
                    if ti + 1 < NT:
                        kblocks.append((ti + 1, mnext))
                    kblocks += [(0, None), (1, None)]
                for g in range(2):
                    E = []
                    for (kb, mk) in kblocks:
                        ps = P.bank()
                        P.mm(ps[:].rearrange("p (j q) -> p j q", j=4), kT[:, g, kb * 128:(kb + 1) * 128],
                             qT[:, 4 * g:4 * g + 4, ti * 128:(ti + 1) * 128], True, True, [kT, qT], [ps])
                        e = Es[ei % len(Es)]
                        ei += 1
                        P.act(e[:], ps[:], AF.Exp, [ps], [e], scale=0.125)
                        if mk is not None:
                            e3 = e[:].rearrange("p (h q) -> p h q", h=4)
                            P.tt("dve", e3, e3, mk[:].unsqueeze(1).to_broadcast([128, 4, 128]), ALU.mult, [e, mk], [e])
                        E.append(e)
                    if stop == "swa_qk":
                        continue
                    pso = P.bank()
                    for hh in range(4):
                        for bi, (kb, mk) in enumerate(kblocks):
                            P.mm(pso[:, hh * 72:hh * 72 + 66], E[bi][:, hh * 128:(hh + 1) * 128], vaug[:, kb, g, 0:66],
                                 bi == 0, bi == len(kblocks) - 1, [E[bi], vaug], [pso])
                    if stop == "swa_pv":
                        continue
                    den, ytok = dens[gi % 2], ytoks[gi % 2]
                    gi += 1
                    po3 = pso[:, 0:288].rearrange("p (h e) -> p h e", h=4)
                    P.tt("dve", den[:], po3[:, :, 64], esink[:, 4 * g:4 * g + 4], ALU.add, [pso, esink], [den])
                    P.recip(den[:], den[:], [den], [den])
                    P.tt("dve", ytok[:].rearrange("p (h d) -> p h d", h=4), po3[:, :, 0:64],
                         den[:].unsqueeze(2).to_broadcast([128, 4, 64]), ALU.mult, [pso, den], [ytok])
                    pst = P.bank()
                    for j in range(2):
                        P.tr(pst[:, j * 128:(j + 1) * 128], ytok[:, j * 128:(j + 1) * 128], ident[:], [ytok, ident], [pst])
                    P.cp("act", yT[:, 2 + 2 * g:4 + 2 * g, ti * 128:(ti + 1) * 128],
                         pst[:, 0:256].rearrange("p (j q) -> p j q", j=2), [pst], [yT])

    def hg_group(l, b, need_ctx):
        CH = 32
        TS = 2 * CH
        NTS = TT // TS
        BLK = 256
        with P.scope():
            wg = P.sb("whg", [128, 8, 1280], BF16)
            load_w(wg, w_hg[l], 8, w_hg)
            hga = [P.sb("hga", [CH, 2 * CH], F32) for _ in range(2)]
            hglast = [P.sb("hglast", [CH, CH], F32) for _ in range(2)]
            hgmask = [P.sb("hgmask", [CH, CH], F32) for _ in range(2)]
            hga_raw = [P.sb("hga_raw", [CH, 2 * CH], F32) for _ in range(2)]
            hglast_raw = [P.sb("hglast_raw", [CH, CH], F32) for _ in range(2)]
            for d in range(2):
                P.dma(hga_raw[d][:], c_hg_a[d], reads=[c_hg_a], writes=[hga_raw[d]])
                P.dma(hglast_raw[d][:], c_hg_last[d], reads=[c_hg_last], writes=[hglast_raw[d]])
                P.dma(hgmask[d][:], c_hg_mask[d], reads=[c_hg_mask], writes=[hgmask[d]])
                P.cp("dve", r_(hga[d][:]), hga_raw[d][:], [hga_raw[d]], [hga[d]])
                P.cp("dve", r_(hglast[d][:]), hglast_raw[d][:], [hglast_raw[d]], [hglast[d]])
            S = [P.sb("hS", [64, 4, 64], F32) for _ in range(2)]
            for d in range(2):
                P.cp("dve", r_(S[d][:].rearrange("p a b -> p (a b)")), zero256[0:64, :], [zero256], [S[d]])
            with P.scope():
                lbr = P.sb("lbr", [128, 512], F32)
                omr = P.sb("omr", [128, 512], F32)
                P.dma(lbr[:].rearrange("p (d x) -> p d x", d=2), LBS[0, :, :, l, :], reads=[LBS], writes=[lbr])
                P.dma(omr[:].rearrange("p (d x) -> p d x", d=2), LBS[1, :, :, l, :], reads=[LBS], writes=[omr])
                sts = [P.sb("hst", [128, 2, 3, 256], F32) for _ in range(2)]
                s1s = [P.sb("hs1", [128, 512], F32) for _ in range(2)]
                s2s = [P.sb("hs2", [128, 512], F32) for _ in range(2)]
                for ti in range(NT):
                    st, s1, s2 = sts[ti % 2], s1s[ti % 2], s2s[ti % 2]
                    psz = P.bank()
                    tok_proj(psz[:, 0:512], psz, wg, 256, 512, ti)
                    psv = P.bank()
                    tok_proj(psv[:, 0:256], psv, wg, 768, 256, ti)
                    P.act(s1[:], psz[:], AF.Sigmoid, [psz], [s1])
                    P.act(s2[:], psz[:], AF.Sigmoid, [psz], [s2], scale=-1.0)
                    P.tt("dve", s1[:], s1[:], omr[:], ALU.mult, [s1, omr], [s1])
                    P.tt("dve", s1[:], s1[:], lbr[:], ALU.add, [s1, lbr], [s1])
                    P.act(st[:, :, 0, :], s1[:].rearrange("p (d x) -> p d x", d=2), AF.Ln, [s1], [(st, 0)])
                    P.tt("pool", st[:, :, 1, :], s2[:].rearrange("p (d x) -> p d x", d=2), omr[:].rearrange("p (d x) -> p d x", d=2),
                         ALU.mult, [s2, omr], [(st, 1)])
                    P.cp("act", st[:, 0, 2, :], psv[:, 0:256], [psv], [(st, 2)])
                    P.cp("dve", st[:, 1, 2, :], psv[:, 0:256], [psv], [(st, 3)])
                    P.dma(HGS[ti * 128:(ti + 1) * 128], st[:], reads=[st], writes=[(HGS, ti)])
            with P.scope():
                def mk(name, shape, n=2):
                    return [P.sb(name, shape, F32) for _ in range(n)]
                qblk, kblk = mk("hqblk", [64, 4, BLK]), mk("hkblk", [64, 4, BLK])
                ld, vtm, lfr = mk("hld", [CH, 2, 3, 256]), mk("hv", [CH, 2, 256]), mk("hlfr", [CH, 2, 256])
                E13, E2, E4 = mk("hE13", [64, 2, 4, 2 * CH]), mk("hE2", [64, 2, 4, CH]), mk("hE4", [CH, 2, 256])
                qq, kt, kh, PT, stmp = mk("hqq", [64, 2, 4, 2, CH]), mk("hkt", [64, 2, 4, CH]), mk("hkh", [CH, 2, 256]), \
                    mk("hPT", [CH, 2, 4, CH]), mk("hstmp", [64, 4, 64])
                nctx = T_CTX // TS
                orders = [list(range(NTS)), list(range(nctx - 1, -1, -1)) + list(range(NTS - 1, nctx - 1, -1))]
                cur_blk = [None, None]

                def load_block(d, blk):
                    c0 = blk * BLK
                    fo = 256 + d * 256
                    for h in range(4):
                        ps = P.bank()
                        for kc in range(8):
                            P.mm(ps[0:64, 0:BLK], wg[:, kc, h * 64:(h + 1) * 64], hT[:, kc, c0:c0 + BLK], kc == 0, kc == 7, [wg, hT], [ps])
                        P.cp("act", qblk[d][:, h, :], ps[0:64, 0:BLK], [ps], [qblk[d]])
                        ps = P.bank()
                        for kc in range(8):
                            P.mm(ps[0:64, 0:BLK], wg[:, kc, fo + h * 64:fo + (h + 1) * 64], hT[:, kc, c0:c0 + BLK], kc == 0, kc == 7,
                                 [wg, hT], [ps])
                        P.act(kblk[d][:, h, :], ps[0:64, 0:BLK], AF.Sigmoid, [ps], [kblk[d]], scale=-1.0)
                    P.tt("dve", kblk[d][:], kblk[d][:], oml_fm[:, d, l, :].unsqueeze(2).to_broadcast([64, 4, BLK]), ALU.mult,
                         [kblk[d], oml_fm], [kblk[d]])

                ots = mk("hot", [64, 4, TS], 4)
                otc = [0]

                def hg_step(d, step):
                        ti = orders[d][step]
                        tok = slice(ti * TS, (ti + 1) * TS)
                        blk = (ti * TS) // BLK
                        off = (ti * TS) % BLK
                        if cur_blk[d] != blk:
                            load_block(d, blk)
                            cur_blk[d] = blk
                        Sd = S[d]
                        i_last = (CH - 1) if d == 0 else 0
                        P.dma(ld[d][:], HGS[ti * TS:(ti + 1) * TS, d].rearrange("(c p) k x -> p c k x", p=CH),
                              reads=[(HGS, (ti * TS) // 128)], writes=[ld[d]])
                        P.cp("pool", r_(vtm[d][:]), ld[d][:, :, 2, :], [ld[d]], [vtm[d]])
                        P.cp("pool", r_(lfr[d][:]), ld[d][:, :, 0, :], [ld[d]], [lfr[d]])
                        yield
                        psA = P.bank()
                        psK = P.bank()
                        for c in range(2):
                            for h in range(4):
                                P.mm(psA[0:64, (c * 4 + h) * 2 * CH:(c * 4 + h + 1) * 2 * CH], r_(lfr[d][:, c, h * 64:(h + 1) * 64]), r_(hga[d][:]),
                                     True, True, [lfr[d], hga[d]], [psA])
                            P.mm(psK[0:CH, c * 256:(c + 1) * 256], r_(hglast[d][:]), r_(lfr[d][:, c, :]), True, True, [hglast[d], lfr[d]], [psK])
                        P.act(E13[d][:].rearrange("p c h x -> p (c h x)"), psA[0:64, 0:16 * CH], AF.Exp, [psA], [E13[d]])
                        P.act(E2[d][:].rearrange("p c h x -> p (c h) x"),
                              psA[0:64, 0:16 * CH].rearrange("p (a x) -> p a x", a=8)[:, :, 0:CH], AF.Exp, [psA], [E2[d]], scale=-1.0)
                        P.act(E4[d][:].rearrange("p c x -> p (c x)"), psK[0:CH, :], AF.Exp, [psK], [E4[d]])
                        yield
                        for c in range(2):
                            cs = slice(off + c * CH, off + (c + 1) * CH)
                            P.tt("dve", r_(qq[d][:, c, :, 0, :]), qblk[d][:, :, cs], E13[d][:, c, :, 0:CH], ALU.mult, [qblk[d], E13[d]], [qq[d]])
                            P.tt("dve", r_(qq[d][:, c, :, 1, :]), qblk[d][:, :, cs], E13[d][:, c, :, CH:2 * CH], ALU.mult, [qblk[d], E13[d]], [qq[d]])
                            P.tt("pool", r_(kt[d][:, c, :, :]), kblk[d][:, :, cs], E2[d][:, c, :, :], ALU.mult, [kblk[d], E2[d]], [kt[d]])
                        P.tt("pool", r_(kh[d][:]), ld[d][:, :, 1, :], E4[d][:], ALU.mult, [ld[d], E4[d]], [kh[d]])
                        yield
                        psS = P.bank()
                        for c in range(2):
                            for h in range(4):
                                P.mm(psS[0:CH, (c * 4 + h) * CH:(c * 4 + h + 1) * CH], r_(kt[d][:, c, h, :]), r_(qq[d][:, c, h, 0, :]),
                                     True, True, [kt[d], qq[d]], [psS])
                        P.tt("dve", r_(PT[d][:].rearrange("p c h i -> p (c h) i")), psS[0:CH, 0:8 * CH].rearrange("p (a i) -> p a i", a=8),
                             hgmask[d][:].unsqueeze(1).to_broadcast([CH, 8, CH]), ALU.mult, [psS, hgmask[d]], [PT[d]])
                        yield
                        pso = P.bank()
                        for c in ([0, 1] if d == 0 else [1, 0]):
                            for h in range(4):
                                o_ap = pso[0:64, h * TS + c * CH:h * TS + (c + 1) * CH]
                                P.mm(o_ap, r_(vtm[d][:, c, h * 64:(h + 1) * 64]), r_(PT[d][:, c, h, :]), True, False, [vtm[d], PT[d]], [pso])
                                P.mm(o_ap, r_(Sd[:, h, :]), r_(qq[d][:, c, h, 1, :]), False, True, [Sd, qq[d]], [pso])
                            psU = P.bank()
                            for h in range(4):
                                P.mm(psU[0:64, h * 64:(h + 1) * 64], r_(kh[d][:, c, h * 64:(h + 1) * 64]), r_(vtm[d][:, c, h * 64:(h + 1) * 64]),
                                     True, True, [kh[d], vtm[d]], [psU])
                            P.tt("dve", stmp[d][:], Sd[:], E13[d][:, c, :, CH + i_last].unsqueeze(2).to_broadcast([64, 4, 64]), ALU.mult,
                                 [Sd, E13[d]], [stmp[d]])
                            P.tt("dve", r_(Sd[:]), stmp[d][:], psU[0:64, 0:256].rearrange("p (h x) -> p h x", h=4), ALU.add,
                                 [stmp[d], psU], [Sd])
                        ot = ots[otc[0] % 4]
                        otc[0] += 1
                        P.cp("act", ot[:].rearrange("p h i -> p (h i)"), pso[0:64, 0:4 * TS], [pso], [ot])
                        P.dma(OS[d, :, :, tok], ot[:], reads=[ot], writes=[(OS, (d, (ti * TS) // 128))])

                run_streams([chain_steps(hg_step, 0, NTS), chain_steps(hg_step, 1, NTS)], lag=4)
            head_out(l, need_ctx, wg, 1024, hgnorm, 6)

    def dn_group(l, b, need_ctx):
        with P.scope():
            wsm = P.sb("wdn_s", [128, 8, 272], BF16)
            for kc in range(8):
                P.dma(wsm[:, kc, :], w_dn[l][kc * 128:(kc + 1) * 128, 768:1040], reads=[w_dn], writes=[(wsm, kc)], queue="pool")
            with P.scope():
                wg = P.sb("wdn", [128, 8, 768], BF16)
                for kc in range(8):
                    P.dma(wg[:, kc, :], w_dn[l][kc * 128:(kc + 1) * 128, 0:768], reads=[w_dn], writes=[(wg, kc)], queue="pool")
                convw = P.sb("convw", [128, 6, 5], F32)
                P.dma(convw[:], dn_conv_in[:, l, :, :], reads=[dn_conv_in], writes=[convw])
                raws = [P.sb("raw", [128, TT], F32) for _ in range(2)]
                cvs = [P.sb("cv", [128, TT], F32) for _ in range(2)]
                stg = [P.sb("stg", [128, 4, 128], F32) for _ in range(2)]
                si = 0
                for ch in range(6):
                    rw, cv = raws[ch % 2], cvs[ch % 2]
                    for (c0, n, r) in blocks512():
                        ps = feat_proj(wg, ch * 128, c0, n)
                        P.cp("act", rw[:, c0:c0 + n], ps[:, :n], [ps], [rw])
                    for (lo, hi) in ((0, T_CTX), (T_CTX, TT)):
                        P.ts("dve", cv[:, lo:hi], rw[:, lo:hi], convw[:, ch, 2:3], None, ALU.mult, None, [rw, convw], [cv])
                        for s in (0, 1, 3, 4):
                            dl = s - 2
                            a, bb = max(lo, lo - dl), min(hi, hi - dl)
                            P.stt(cv[:, a:bb], rw[:, a + dl:bb + dl], convw[:, ch, s:s + 1], cv[:, a:bb], ALU.mult, ALU.add,
                                  [rw, convw, cv], [cv])
                    P.act(cv[:], cv[:], AF.Silu, [cv], [cv])
                    for t0 in range(0, NT, 4):
                        nt = min(4, NT - t0)
                        ps = P.bank()
                        for k in range(nt):
                            P.tr(ps[:, k * 128:(k + 1) * 128], cv[:, (t0 + k) * 128:(t0 + k + 1) * 128], ident[:], [cv, ident], [ps])
                        st = stg[si % 2]
                        si += 1
                        P.cp("act" if si % 2 else "dve", st[:, 0:nt, :], ps[:, 0:nt * 128].rearrange("p (k c) -> p k c", k=nt),
                             [ps], [st])
                        P.dma(QKV[t0 * 128:(t0 + nt) * 128, ch * 128:(ch + 1) * 128].rearrange("(k p) c -> p k c", p=128),
                              st[:, 0:nt, :], reads=[st], writes=[(QKV, (t0, ch))])
            with P.scope():
                dnm = [P.sb("dnm", [128, 3, 128], F32) for _ in range(2)]
                dnl = [P.sb("dnl", [128, 7, 128], BF16) for _ in range(2)]
                for d in range(2):
                    P.dma(dnm[d][:], c_dn_masks[d, 0:3].rearrange("m p c -> p m c"), reads=[c_dn_masks], writes=[dnm[d]])
                    P.dma(dnl[d][:], c_dn_masks[d, 3:10].rearrange("m p c -> p m c"), reads=[c_dn_masks], writes=[dnl[d]], queue="pool")
                negA = P.sb("negA", [128, 8], F32)
                dtb = P.sb("dtb", [128, 8], F32)
                P.dma(negA[:], dn_alog_in[:, l * 8:(l + 1) * 8], reads=[dn_alog_in], writes=[negA])
                P.dma(dtb[:], dn_dtb_in[:, l * 8:(l + 1) * 8], reads=[dn_dtb_in], writes=[dtb])
                P.act(negA[:], negA[:], AF.Exp, [negA], [negA])
                P.ts("dve", negA[:], negA[:], -1.0, None, ALU.mult, None, [negA], [negA])
                S = [P.sb("dS", [64, 4, 64], F32) for _ in range(2)]
                for d in range(2):
                    P.cp("dve", r_(S[d][:].rearrange("p a b -> p (a b)")), zero256[0:64, :], [zero256], [S[d]])

                def mk(name, shape, n=2):
                    return [P.sb(name, shape, F32) for _ in range(n)]
                qkvt = mk("dqkv", [128, 768], 2)
                sqt, ss, rs, xa, ax, sp, gg, beta = mk("dsq", [128, 512]), mk("dss", [128, 8]), mk("drs", [128, 8]), \
                    mk("dxa", [128, 8]), mk("dax", [128, 8]), mk("dsp", [128, 8]), mk("dg", [128, 8]), mk("dbeta", [128, 8])
                qn, kn, knT, qnT = mk("dqn", [128, 4, 64]), mk("dkn", [128, 4, 64]), mk("dknT", [64, 4, 128]), mk("dqnT", [64, 4, 128])
                gcs, egc, elast, etot = mk("dgcs", [128, 8]), mk("degc", [128, 4]), mk("delast", [128, 4]), mk("detot", [128, 4])
                gm, DMT, AT, scTt, bt, Zb = mk("dgm", [128, 4, 128], 2), mk("dDMT", [128, 512], 2), mk("dAT", [128, 512], 2), \
                    mk("dscT", [128, 512], 2), mk("dbt", [128, 512], 2), mk("dZ", [128, 512], 2)
                Tb2 = [mk("dT", [128, 512], 2) for _ in range(2)]
                TTb2 = [mk("dTT", [128, 512], 2) for _ in range(2)]
                ots = mk("dot", [64, 4, 128], 2)
                kb, kbg, vb, qh, khat = mk("dkb", [128, 4, 64]), mk("dkbg", [128, 4, 64]), mk("dvb", [128, 4, 64]), \
                    mk("dqh", [128, 4, 64]), mk("dkhat", [128, 4, 64])
                kbT, qhT, ub, wTb, vnb, stmp = mk("dkbT", [64, 4, 128]), mk("dqhT", [64, 4, 128]), mk("du", [128, 256]), \
                    mk("dwT", [64, 4, 128]), mk("dvn", [128, 256]), mk("dstmp", [64, 4, 64])
                identb4 = ident[:].unsqueeze(1).to_broadcast([128, 4, 128])

                def tr_heads(dst, src, src_res):
                    pst = P.bank()
                    for h in range(4):
                        P.tr(pst[0:64, h * 128:(h + 1) * 128], src[:, h, :], ident[:], [src_res, ident], [pst])
                    P.cp("act", r_(dst[:].rearrange("p h t -> p (h t)")), pst[0:64, :], [pst], [dst])

                def prep_tile(ti, slot):
                    tok = slice(ti * 128, (ti + 1) * 128)
                    s_ = slot % 2
                    qk = qkvt[s_]
                    P.dma(qk[:], QKV[tok, :], reads=[QKV], writes=[qk])
                    psab = P.bank()
                    for kc in range(8):
                        P.mm(psab[:, 0:16], hT[:, kc, tok], wsm[:, kc, 256:272], kc == 0, kc == 7, [wsm, hT], [psab])
                    P.tt("dve", xa[s_][:], psab[:, 0:8], dtb[:], ALU.add, [psab, dtb], [xa[s_]])
                    P.stt(ax[s_][:], xa[s_][:], -1.0, xa[s_][:], ALU.mult, ALU.max, [xa[s_]], [ax[s_]])
                    P.act(ax[s_][:], ax[s_][:], AF.Exp, [ax[s_]], [ax[s_]], scale=-1.0)
                    P.act(ax[s_][:], ax[s_][:], AF.Ln, [ax[s_], oneb], [ax[s_]], bias=oneb[:, 0:1])
                    P.stt(sp[s_][:], xa[s_][:], 0.0, ax[s_][:], ALU.max, ALU.add, [xa[s_], ax[s_]], [sp[s_]])
                    P.tt("dve", gg[s_][:], sp[s_][:], negA[:], ALU.mult, [sp[s_], negA], [gg[s_]])
                    P.act(beta[s_][:], psab[:, 8:16], AF.Exp, [psab], [beta[s_]], scale=-1.0)
                    P.ts("dve", beta[s_][:], beta[s_][:], 1.0, None, ALU.add, None, [beta[s_]], [beta[s_]])
                    P.recip(beta[s_][:], beta[s_][:], [beta[s_]], [beta[s_]])
                    P.tt("pool", sqt[s_][:], qk[:, 0:512], qk[:, 0:512], ALU.mult, [qk], [sqt[s_]])
                    P.red(ss[s_][:], sqt[s_][:].rearrange("p (h d) -> p h d", d=64), ALU.add, [sqt[s_]], [ss[s_]])
                    rstd_from_sumsq(ss[s_][:], rs[s_][:], 1.0, ss[s_], rs[s_])
                    P.ts("dve", rs[s_][:, 0:4], rs[s_][:, 0:4], 0.125, None, ALU.mult, None, [rs[s_]], [rs[s_]])
                    q3 = qk[:, 0:256].rearrange("p (h d) -> p h d", d=64)
                    k3 = qk[:, 256:512].rearrange("p (h d) -> p h d", d=64)
                    P.tt("dve", qn[s_][:], q3, rs[s_][:, 0:4].unsqueeze(2).to_broadcast([128, 4, 64]), ALU.mult, [qk, rs[s_]], [qn[s_]])
                    P.tt("dve", kn[s_][:], k3, rs[s_][:, 4:8].unsqueeze(2).to_broadcast([128, 4, 64]), ALU.mult, [qk, rs[s_]], [kn[s_]])
                    tr_heads(knT[s_], kn[s_], kn[s_])
                    tr_heads(qnT[s_], qn[s_], qn[s_])
                    return qk, s_

                orders = [scan_order(0), scan_order(1)]
                def dn_step(d, step):
                        ti = orders[d][step]
                        qk, s_ = prep_tile(ti, 2 * step + d)
                        Tb, TTb = Tb2[d], TTb2[d]
                        yield
                        tok = slice(ti * 128, (ti + 1) * 128)
                        m = dnm[d]
                        ml_ = dnl[d]
                        Sd = S[d]
                        g_d = gg[s_][:, d * 4:(d + 1) * 4]
                        b_d = beta[s_][:, d * 4:(d + 1) * 4]
                        psg = P.bank()
                        P.mm(psg[:, 0:4], m[:, 0, :], g_d, True, True, [m, gg[s_]], [psg])
                        P.mm(psg[:, 4:8], ones_f[:], g_d, True, True, [ones_f, gg[s_]], [psg])
                        P.cp("dve", gcs[d][:], psg[:, 0:8], [psg], [gcs[d]])
                        P.act(egc[d][:], gcs[d][:, 0:4], AF.Exp, [gcs[d]], [egc[d]])
                        P.tt("dve", elast[d][:], gcs[d][:, 4:8], gcs[d][:, 0:4], ALU.subtract, [gcs[d]], [elast[d]])
                        P.act(elast[d][:], elast[d][:], AF.Exp, [elast[d]], [elast[d]])
                        P.act(etot[d][:], gcs[d][:, 4:8], AF.Exp, [gcs[d]], [etot[d]])
                        P.tt("dve", gm[d][:], m[:, 0, :].unsqueeze(1).to_broadcast([128, 4, 128]),
                             g_d.unsqueeze(2).to_broadcast([128, 4, 128]), ALU.mult, [m, gg[s_]], [gm[d]])
                        psD = P.bank()
                        for h in range(4):
                            P.mm(psD[:, h * 128:(h + 1) * 128], m[:, 1, :], gm[d][:, h, :], True, False, [m, gm[d]], [psD])
                            P.mm(psD[:, h * 128:(h + 1) * 128], ident[:], m[:, 2, :], False, True, [ident, m], [psD])
                        P.act(DMT[d][:], psD[:], AF.Exp, [psD], [DMT[d]])
                        yield
                        v3 = qk[:, 512:768].rearrange("p (h d) -> p h d", d=64)
                        bb4 = b_d.unsqueeze(2).to_broadcast([128, 4, 64])
                        eg4 = egc[d][:].unsqueeze(2).to_broadcast([128, 4, 64])
                        el4 = elast[d][:].unsqueeze(2).to_broadcast([128, 4, 64])
                        P.tt("pool", kb[d][:], kn[s_][:], bb4, ALU.mult, [kn[s_], beta[s_]], [kb[d]])
                        P.tt("pool", r_(kbg[d][:]), kb[d][:], eg4, ALU.mult, [kb[d], egc[d]], [kbg[d]])
                        P.tt("pool", r_(vb[d][:]), v3, bb4, ALU.mult, [qk, beta[s_]], [vb[d]])
                        P.tt("pool", qh[d][:], qn[s_][:], eg4, ALU.mult, [qn[s_], egc[d]], [qh[d]])
                        P.tt("pool", r_(khat[d][:]), kn[s_][:], el4, ALU.mult, [kn[s_], elast[d]], [khat[d]])
                        tr_heads(kbT[d], kb[d], kb[d])
                        tr_heads(qhT[d], qh[d], qh[d])
                        yield
                        psG = P.bank()
                        psQ = P.bank()
                        for h in range(4):
                            P.mm(psG[:, h * 128:(h + 1) * 128], r_(knT[s_][:, h, :]), r_(kbT[d][:, h, :]), True, True,
                                 [knT[s_], kbT[d]], [psG])
                            P.mm(psQ[:, h * 128:(h + 1) * 128], r_(knT[s_][:, h, :]), r_(qnT[s_][:, h, :]), True, True,
                                 [knT[s_], qnT[s_]], [psQ])
                        P.tt("dve", AT[d][:], psG[:], DMT[d][:], ALU.mult, [psG, DMT[d]], [AT[d]])
                        P.tt("dve", r_(scTt[d][:]), psQ[:], DMT[d][:], ALU.mult, [psQ, DMT[d]], [scTt[d]])
                        yield
                        A3 = AT[d][:].rearrange("p (h i) -> p h i", h=4)
                        b3 = bt[d][:].rearrange("p (h i) -> p h i", h=4)

                        def lvl_mask(li):
                            return ml_[:, li, :].unsqueeze(1).to_broadcast([128, 4, 128])
                        cur, nxt = 0, 1
                        P.tt("pool", r_(b3), A3, lvl_mask(0), ALU.mult, [AT[d], ml_], [bt[d]])
                        P.tt("pool", r_(TTb[cur][:].rearrange("p (h i) -> p h i", h=4)), identb4, b3, ALU.subtract,
                             [ident, bt[d]], [TTb[cur]])
                        psT = P.bank()
                        for h in range(4):
                            P.tr(psT[:, h * 128:(h + 1) * 128], TTb[cur][:, h * 128:(h + 1) * 128], ident[:], [TTb[cur], ident], [psT])
                        P.cp("act", r_(Tb[cur][:]), psT[:], [psT], [Tb[cur]])
                        for li in range(1, 7):
                            P.tt("pool", r_(b3), A3, lvl_mask(li), ALU.mult, [AT[d], ml_], [bt[d]])
                            psZ = P.bank()
                            for h in range(4):
                                hs_ = slice(h * 128, (h + 1) * 128)
                                P.mm(psZ[:, hs_], r_(bt[d][:, hs_]), r_(Tb[cur][:, hs_]), True, True, [bt[d], Tb[cur]], [psZ])
                            P.cp("act", r_(Zb[d][:]), psZ[:], [psZ], [Zb[d]])
                            if li < 6:
                                psW = P.bank()
                                for h in range(4):
                                    hs_ = slice(h * 128, (h + 1) * 128)
                                    P.mm(psW[:, hs_], r_(TTb[cur][:, hs_]), r_(Zb[d][:, hs_]), True, True, [TTb[cur], Zb[d]], [psW])
                                P.tt("dve", r_(Tb[nxt][:]), Tb[cur][:], psW[:], ALU.subtract, [Tb[cur], psW], [Tb[nxt]])
                            psWT = P.bank()
                            for h in range(4):
                                hs_ = slice(h * 128, (h + 1) * 128)
                                P.mm(psWT[:, hs_], r_(Zb[d][:, hs_]), r_(TTb[cur][:, hs_]), True, True, [Zb[d], TTb[cur]], [psWT])
                            P.tt("dve", r_(TTb[nxt][:]), TTb[cur][:], psWT[:], ALU.subtract, [TTb[cur], psWT], [TTb[nxt]])
                            cur, nxt = nxt, cur
                            yield
                        TTf = TTb[cur]
                        psU = P.bank()
                        psWt = P.bank()
                        for h in range(4):
                            hs_ = slice(h * 128, (h + 1) * 128)
                            P.mm(psU[:, h * 64:(h + 1) * 64], r_(TTf[:, hs_]), r_(vb[d][:, h, :]), True, True, [TTf, vb[d]], [psU])
                            P.mm(psWt[0:64, hs_], r_(kbg[d][:, h, :]), r_(TTf[:, hs_]), True, True, [kbg[d], TTf], [psWt])
                        P.cp("act", ub[d][:], psU[:, 0:256], [psU], [ub[d]])
                        P.cp("act", r_(wTb[d][:].rearrange("p h t -> p (h t)")), psWt[0:64, :], [psWt], [wTb[d]])
                        yield
                        psV = P.bank()
                        for h in range(4):
                            P.mm(psV[:, h * 64:(h + 1) * 64], r_(wTb[d][:, h, :]), r_(Sd[:, h, :]), True, True, [wTb[d], Sd], [psV])
                        P.tt("dve", r_(vnb[d][:]), ub[d][:], psV[:, 0:256], ALU.subtract, [ub[d], psV], [vnb[d]])
                        psO = P.bank()
                        psS = P.bank()
                        for h in range(4):
                            o_ap = psO[0:64, h * 128:(h + 1) * 128]
                            P.mm(o_ap, r_(Sd[:, h, :]), r_(qhT[d][:, h, :]), True, False, [Sd, qhT[d]], [psO])
                            P.mm(o_ap, r_(vnb[d][:, h * 64:(h + 1) * 64]), r_(scTt[d][:, h * 128:(h + 1) * 128]), False, True,
                                 [vnb[d], scTt[d]], [psO])
                            P.mm(psS[0:64, h * 64:(h + 1) * 64], r_(khat[d][:, h, :]), r_(vnb[d][:, h * 64:(h + 1) * 64]), True, True,
                                 [khat[d], vnb[d]], [psS])
                        P.cp("act", ots[d][:].rearrange("p h i -> p (h i)"), psO[0:64, :], [psO], [ots[d]])
                        P.dma(OS[d, :, :, tok], ots[d][:], reads=[ots[d]], writes=[(OS, (d, ti))])
                        P.tt("dve", stmp[d][:], Sd[:], etot[d][0:64, :].unsqueeze(2).to_broadcast([64, 4, 64]), ALU.mult,
                             [Sd, etot[d]], [stmp[d]])
                        P.tt("dve", r_(Sd[:]), stmp[d][:], psS[0:64, 0:256].rearrange("p (h x) -> p h x", h=4), ALU.add,
                             [stmp[d], psS], [Sd])

                run_streams([chain_steps(dn_step, 0, NT), chain_steps(dn_step, 1, NT)], lag=7)
            head_out(l, need_ctx, wsm, 0, dnnorm, 0)

    with P.scope():
        xin = [P.sb("xin", [128, D], F32) for _ in range(2)]
        xo = [P.sb("xo", [128, 8, 128], F32) for _ in range(2)]
        k = 0
        for b in range(2):
            for ti in range(NT):
                src = ctx_in[b, ti * 128:(ti + 1) * 128, :] if ti < 2 else x_in[b, (ti - 2) * 128:(ti - 1) * 128, :]
                src_res = ctx_in if ti < 2 else x_in
                xi = xin[k % 2]
                xt = xo[k % 2]
                k += 1
                P.dma(xi[:], src, reads=[src_res], writes=[xi])
                for hb in range(2):
                    ps = P.bank()
                    for c4 in range(4):
                        c = hb * 4 + c4
                        P.tr(ps[:, c4 * 128:(c4 + 1) * 128], xi[:, c * 128:(c + 1) * 128], ident[:], [xi, ident], [ps])
                    P.cp("act" if hb else "dve", xt[:, hb * 4:hb * 4 + 4, :],
                         ps[:].rearrange("p (c t) -> p c t", c=4), [ps], [(xt, hb)])
                P.dma(XT[b, :, :, ti * 128:(ti + 1) * 128], xt[:], reads=[xt], writes=[(XT, (b, ti))])

    with P.scope():
        for (inner, src, NP) in ((256, lb_rep_in, 128), (4, lb_fm_in, 64)):
            raw = P.sb("lbraw", [NP, 2, DEPTH, inner], F32)
            lb = P.sb("lbt", [NP, 2, DEPTH, inner], F32)
            oml = P.sb("omlt", [NP, 2, DEPTH, inner], F32)
            mx = P.sb("lbmx", [NP, 2, inner], F32)
            sm = P.sb("lbsm", [NP, 2, inner], F32)
            srcap = src[:].rearrange("p (d l c) -> p d l c", d=2, l=DEPTH) if inner == 256 else src[:]
            P.dma(raw[:], srcap, reads=[src], writes=[raw])
            P.cp("dve", mx[:], raw[:, :, 0, :], [raw], [mx])
            for ll in range(1, DEPTH):
                P.tt("dve", mx[:], mx[:], raw[:, :, ll, :], ALU.max, [mx, raw], [mx])
            for ll in range(DEPTH):
                P.tt("dve", raw[:, :, ll, :], raw[:, :, ll, :], mx[:], ALU.subtract, [raw, mx], [raw])
            P.act(raw[:], raw[:], AF.Exp, [raw], [raw])
            P.cp("dve", sm[:], raw[:, :, 0, :], [raw], [sm])
            for ll in range(1, DEPTH):
                P.tt("dve", sm[:], sm[:], raw[:, :, ll, :], ALU.add, [sm, raw], [sm])
            P.recip(sm[:], sm[:], [sm], [sm])
            for ll in range(DEPTH):
                P.tt("dve", raw[:, :, ll, :], raw[:, :, ll, :], sm[:], ALU.mult, [raw, sm], [raw])
            P.memset("dve", lb[:, :, 0, :], 0.0, [lb])
            for ll in range(1, DEPTH):
                P.tt("dve", lb[:, :, ll, :], lb[:, :, ll - 1, :], raw[:, :, ll, :], ALU.add, [lb, raw], [lb])
            P.ts("dve", oml[:], lb[:], -1.0, 1.0, ALU.mult, ALU.add, [lb], [oml])
            if inner == 256:
                P.dma(LBS[0], lb[:], reads=[lb], writes=[LBS])
                P.dma(LBS[1], oml[:], reads=[oml], writes=[LBS])
            else:
                P.cp("dve", lbfm[:], lb[:], [lb], [lbfm])
                P.cp("dve", oml_fm[:], oml[:], [oml], [oml_fm])

    for l in range(L):
        need_ctx = not (last_global and l == DEPTH - 1)
        with P.scope():
            wa = [P.sb("wa", [128, 8, 768], F32) for _ in range(2)]
            mv = modv[:].rearrange("p m c r -> p (m c) r")
            for blk in range(8):
                w = wa[blk % 2]
                src = w_ada[l, :, blk * 768:(blk + 1) * 768].rearrange("(kc p) n -> p kc n", p=128)
                P.dma(w[:], src, reads=[w_ada], writes=[w])
                ps = P.bank()
                for j in range(6):
                    for kc in range(8):
                        P.mm(ps[:, j * 3:(j + 1) * 3], w[:, kc, j * 128:(j + 1) * 128], scT[:, kc, :],
                             kc == 0, kc == 7, [w, scT], [ps])
                P.tt("dve", mv[:, blk * 6:(blk + 1) * 6, :], ps[:, 0:18].rearrange("p (j r) -> p j r", r=3),
                     bada[:, l, blk * 6:(blk + 1) * 6].unsqueeze(2).to_broadcast([128, 6, 3]), ALU.add,
                     [ps, bada], [(modv, blk)])
            for (g, mi, nn) in ((g1, 1, n1), (g2, 4, n2)):
                P.ts("dve", g[:], modv[:, mi, :, :], 1.0, None, ALU.add, None, [modv], [g])
                P.tt("dve", g[:], g[:], nn[:, l, :].unsqueeze(2).to_broadcast([128, 8, 3]), ALU.mult, [g, nn], [g])

        mix_scope = P.scope()
        mix_scope.__enter__()
        hT = P.sb("hT", [128, 8, TT], BF16)
        yT = P.sb("yT", [128, 8, TT], BF16)
        P.memset("pool", yT[:].rearrange("p a t -> p (a t)"), 0.0, [yT])
        for b in range(2):
            with P.scope():
                xbs = [P.sb("xb", [128, 8, 512], F32) for _ in range(2)]
                sqs = [P.sb("sq", [128, 8, 512], BF16) for _ in range(2)]
                rss = [P.sb("rs", [128, 512], F32) for _ in range(2)]
                for bi, (c0, n, r) in enumerate(blocks512()):
                    r = b if r is None else r
                    xb, sq, rs = xbs[bi % 2], sqs[bi % 2], rss[bi % 2]
                    P.dma(xb[:, :, :n], XT[b, :, :, c0:c0 + n], reads=xk(b, c0, n), writes=[xb])
                    P.act(sq[:, :, :n], xb[:, :, :n], AF.Square, [xb], [sq])
                    ps = P.bank()
                    for c in range(8):
                        P.mm(ps[:, :n], ones_bf[:], sq[:, c, :n], c == 0, c == 7, [ones_bf, sq], [ps])
                    rstd_from_sumsq(ps[:, :n], rs[:, :n], 1.0 / D, ps, rs)
                    P.tt("dve", xb[:, :, :n], xb[:, :, :n], rs[:, :n].unsqueeze(1).to_broadcast([128, 8, n]), ALU.mult,
                         [xb, rs], [xb])
                    for c in range(8):
                        P.act(hT[:, c, c0:c0 + n], xb[:, c, :n], AF.Identity, [xb, g1, modv], [hT],
                              scale=g1[:, c, r:r + 1], bias=modv[:, 0, c, r:r + 1])
                if dbg and l == 0 and b == 0:
                    d_h = dbg_tensor("hT", [128, 8, TT])
                    hf = P.sb("hf", [128, 8, TT // 2], F32)
                    for hh in range(2):
                        P.cp("dve", hf[:], hT[:, :, hh * (TT // 2):(hh + 1) * (TT // 2)], [hT], [hf])
                        P.dma(d_h[:, :, hh * (TT // 2):(hh + 1) * (TT // 2)], hf[:], reads=[hf], writes=[d_h])

            if "swa" in groups:
                swa_group(l, b, need_ctx)
            if "hg" in groups:
                hg_group(l, b, need_ctx)
            if "dn" in groups:
                dn_group(l, b, need_ctx)

            if dbg and l == 0 and b == 0:
                d_y = dbg_tensor("yT", [128, 8, TT])
                with P.scope():
                    yf = P.sb("yf", [128, 8, TT // 2], F32)
                    for hh in range(2):
                        P.cp("dve", yf[:], yT[:, :, hh * (TT // 2):(hh + 1) * (TT // 2)], [yT], [yf])
                        P.dma(d_y[:, :, hh * (TT // 2):(hh + 1) * (TT // 2)], yf[:], reads=[yf], writes=[d_y])

            with P.scope():
                wout = P.sb("wout", [128, 8, D], BF16)
                load_w(wout, w_out[l], 8, w_out)
                xbs = [P.sb("xb", [128, 8, 512], F32) for _ in range(2)]
                for bi, (c0, n, r) in enumerate(blocks512()):
                    if r == 2 and not need_ctx:
                        continue
                    r = b if r is None else r
                    xb = xbs[bi % 2]
                    P.dma(xb[:, :, :n], XT[b, :, :, c0:c0 + n], reads=xk(b, c0, n), writes=[xb])
                    for c in range(8):
                        ps = P.bank()
                        for kc in range(8):
                            P.mm(ps[:, :n], wout[:, kc, c * 128:(c + 1) * 128], yT[:, kc, c0:c0 + n], kc == 0, kc == 7,
                                 [wout, yT], [ps])
                        P.stt(xb[:, c, :n], ps[:, :n], modv[:, 2, c, r:r + 1], xb[:, c, :n], ALU.mult, ALU.add,
                              [ps, modv, xb], [xb])
                    P.dma(XT[b, :, :, c0:c0 + n], xb[:, :, :n], reads=[xb], writes=xk(b, c0, n))

        mix_scope.__exit__(None, None, None)
        with P.scope():
            w1 = P.sb("w1", [128, 8, 4 * D], BF16)
            w2 = P.sb("w2", [128, 32, D], BF16)
            for cb in range(8):
                P.dma(w1[:, :, cb * 512:(cb + 1) * 512],
                      w_ff1[l][:, cb * 512:(cb + 1) * 512].rearrange("(kc p) n -> p kc n", p=128),
                      reads=[w_ff1], writes=[(w1, cb)], queue="pool")
            load_w(w2, w_ff2[l], 32, w_ff2)
            xbs = [P.sb("xb", [128, 8, 256], F32) for _ in range(2)]
            sqs = [P.sb("sq", [128, 8, 256], BF16) for _ in range(2)]
            rss = [P.sb("rs", [128, 256], F32) for _ in range(2)]
            tfs = [P.sb("tf", [128, 256], F32) for _ in range(2)]
            h2s = [P.sb("h2", [128, 8, 256], BF16) for _ in range(2)]
            Hs = [P.sb("H", [128, 32, 256], BF16) for _ in range(1)]
            rl = [P.sb("rl", [128, 256], BF16) for _ in range(3)]
            ot = [P.sb("ot", [128, D], F32) for _ in range(2)] if (last_global and l == DEPTH - 1) else None
            bi = 0
            for b in range(2):
                for blk in range(9):
                    c0, n = blk * 256, 256
                    r = 2 if blk == 0 else b
                    if blk == 0 and not need_ctx:
                        continue
                    xb, sq, rs, h2, H = xbs[bi % 2], sqs[bi % 2], rss[bi % 2], h2s[bi % 2], Hs[0]
                    bi += 1
                    P.dma(xb[:], XT[b, :, :, c0:c0 + n], reads=xk(b, c0, n), writes=[xb])
                    P.act(sq[:], xb[:], AF.Square, [xb], [sq])
                    ps = P.bank()
                    for c in range(8):
                        P.mm(ps[:, :n], ones_bf[:], sq[:, c, :], c == 0, c == 7, [ones_bf, sq], [ps])
                    rstd_from_sumsq(ps[:, :n], rs[:], 1.0 / D, ps, rs)
                    for c in range(8):
                        tf = tfs[c % 2]
                        P.stt(tf[:], xb[:, c, :], g2[:, c, r:r + 1], rs[:], ALU.mult, ALU.mult, [xb, g2, rs], [tf])
                        P.act(h2[:, c, :], tf[:], AF.Identity, [tf, modv], [h2], bias=modv[:, 3, c, r:r + 1])
                    for f in range(32):
                        ps = P.bank()
                        for kc in range(8):
                            P.mm(ps[:, :n], w1[:, kc, f * 128:(f + 1) * 128], h2[:, kc, :], kc == 0, kc == 7, [w1, h2], [ps])
                        t = rl[f % 3]
                        P.act(t[:], ps[:, :n], AF.Relu, [ps], [t])
                        P.tt("pool", H[:, f, :], t[:], t[:], ALU.mult, [t], [(H, f)])
                    for c in range(8):
                        ps = P.bank()
                        for f in range(32):
                            P.mm(ps[:, :n], w2[:, f, c * 128:(c + 1) * 128], H[:, f, :], f == 0, f == 31, [w2, H], [ps])
                        P.stt(xb[:, c, :], ps[:, :n], modv[:, 5, c, r:r + 1], xb[:, c, :], ALU.mult, ALU.add,
                              [ps, modv, xb], [xb])
                    if last_global and l == DEPTH - 1:
                        P.act(sq[:], xb[:], AF.Square, [xb], [sq])
                        ps = P.bank()
                        for c in range(8):
                            P.mm(ps[:, :n], ones_bf[:], sq[:, c, :], c == 0, c == 7, [ones_bf, sq], [ps])
                        rstd_from_sumsq(ps[:, :n], rs[:], 1.0 / D, ps, rs)
                        for c in range(8):
                            P.stt(xb[:, c, :], xb[:, c, :], nf[:, c:c + 1], rs[:], ALU.mult, ALU.mult, [xb, nf, rs], [xb])
                        for th in range(2):
                            o = ot[th]
                            for hb in range(2):
                                ps = P.bank()
                                for c4 in range(4):
                                    c = hb * 4 + c4
                                    P.tr(ps[:, c4 * 128:(c4 + 1) * 128], xb[:, c, th * 128:(th + 1) * 128], ident[:],
                                         [xb, ident], [ps])
                                P.cp("act" if hb else "dve", o[:, hb * 512:(hb + 1) * 512], ps[:], [ps], [(o, hb)])
                            t0 = c0 - T_CTX + th * 128
                            P.dma(out_d[b, t0:t0 + 128, :], o[:], reads=[o], writes=[(out_d, (b, t0))])
                    else:
                        P.dma(XT[b, :, :, c0:c0 + n], xb[:], reads=[xb], writes=xk(b, c0, n))

    if not last_global:
        with P.scope():
            xbs = [P.sb("xb", [128, 8, 128], F32) for _ in range(2)]
            ot = [P.sb("ot", [128, D], F32) for _ in range(2)]
            k = 0
            for b in range(2):
                for ti in range(2, NT):
                    xb, o = xbs[k % 2], ot[k % 2]
                    k += 1
                    P.dma(xb[:], XT[b, :, :, ti * 128:(ti + 1) * 128], reads=xk(b, ti * 128, 128), writes=[xb])
                    for hb in range(2):
                        ps = P.bank()
                        for c4 in range(4):
                            c = hb * 4 + c4
                            P.tr(ps[:, c4 * 128:(c4 + 1) * 128], xb[:, c, :], ident[:], [xb, ident], [ps])
                        P.cp("act" if hb else "dve", o[:, hb * 512:(hb + 1) * 512], ps[:], [ps], [(o, hb)])
                    P.dma(out_d[b, (ti - 2) * 128:(ti - 1) * 128, :], o[:], reads=[o], writes=[(out_d, (b, ti))])

    P.final_wait()
    P.emit()
    return nc, P, dbg_out


_SHARED_KEYS = ["w_ada", "b_ada", "norm1", "norm2", "norm_f", "w_dn", "w_sw", "w_hg", "w_out", "w_ff1", "w_ff2",
                "dn_conv", "dn_alog", "dn_dtb", "sink", "dn_norm", "hg_norm", "lb_rep", "lb_fm",
                "ident", "perm", "blk64", "cosT", "sinsT", "swa_prev", "swa_next", "dn_masks", "hg_a", "hg_last", "hg_mask", "ones64"]


def make_in_maps(inp, n_cores=8, WL=DEPTH):
    shared = _prep_shared(inp)
    for nm in ["w_ada", "w_dn", "w_sw", "w_hg", "w_out", "w_ff1", "w_ff2"]:
        shared[nm] = np.ascontiguousarray(shared[nm][:WL])
    x = np.asarray(inp["x"], np.float32)
    ctx = np.asarray(inp["ctx"], np.float32)
    c = np.asarray(inp["c"], np.float32)
    c_ctx = np.asarray(inp["c_ctx"], np.float32)
    maps = []
    for i in range(n_cores):
        m = {k: shared[k] for k in _SHARED_KEYS}
        m["x"] = np.ascontiguousarray(x[2 * i:2 * i + 2])
        m["ctx"] = np.ascontiguousarray(ctx[2 * i:2 * i + 2])
        rows = np.stack([c[2 * i], c[2 * i + 1], c_ctx], axis=1)
        m["cT"] = np.ascontiguousarray(rows.reshape(8, 128, 3).transpose(1, 0, 2))
        maps.append(m)
    return maps


def kernel(**inputs):
    inp = {k: np.asarray(v) for k, v in inputs.items()}
    nc, P, _ = build_program(DEPTH)
    maps = make_in_maps(inp, 8)
    res = run_bass_kernel_spmd(nc, maps, core_ids=list(range(8)))
    out = np.concatenate([np.asarray(r["out"], np.float32) for r in res.results], axis=0)
    return out
```

```python
import contextlib
import numpy as np
import ml_dtypes
import concourse.bass as bass
import concourse.mybir as mybir
from concourse.bass_utils import run_bass_kernel_spmd

F32 = mybir.dt.float32
F32R = mybir.dt.float32r
BF16 = mybir.dt.bfloat16
AF = mybir.ActivationFunctionType
ALU = mybir.AluOpType
AX = mybir.AxisListType

SAME_ENGINE_SYNC = True

D = 1024
T_LAT = 2048
T_CTX = 256
TT = T_LAT + T_CTX
NT = TT // 128
DEPTH = 4
EPS = 1e-6
NEG = -30000.0


class Res:
    __slots__ = ("name", "t", "w", "r")

    def __init__(self, name, t):
        self.name = name
        self.t = t
        self.w = {}
        self.r = {}

    def __getitem__(self, k):
        return self.t[k]


class Prog:
    ENGS = ("pe", "act", "dve", "pool", "sp")

    def __init__(self, nc, n_dma_sems=16):
        self.nc = nc
        self.ops = {e: [] for e in self.ENGS}
        self.cnt = {e: 0 for e in self.ENGS}
        self.sem = {e: nc.alloc_semaphore("c_" + e) for e in self.ENGS}
        self.dsem = [nc.alloc_semaphore("d%d" % i) for i in range(n_dma_sems)]
        self.dval = [0] * n_dma_sems
        half = n_dma_sems // 2
        self.dq = {"sp": list(range(0, half)), "pool": list(range(half, n_dma_sems)), "act": []}
        self.dnext = {"sp": 0, "pool": 0, "act": 0}
        self.waited = {}
        self.n_inst = 0
        self.scopes = []
        self.banks = []
        self.bank_i = 0
        self.uid = 0

    def sb(self, name, shape, dtype):
        self.uid += 1
        nm = "%s_%d" % (name, self.uid)
        if self.scopes:
            t = self.scopes[-1].enter_context(self.nc.sbuf_tensor(nm, list(shape), dtype))
        else:
            t = self.nc.alloc_sbuf_tensor(nm, list(shape), dtype)
        return Res(nm, t)

    @contextlib.contextmanager
    def scope(self):
        st = contextlib.ExitStack()
        self.scopes.append(st)
        try:
            yield
            self.barrier()
        finally:
            self.scopes.pop()
            st.close()

    def dram(self, name, shape, dtype, kind="Internal"):
        return Res(name, self.nc.dram_tensor(name, list(shape), dtype, kind=kind))

    def init_banks(self, n=8):
        for i in range(n):
            self.banks.append(Res("bank%d" % i, self.nc.alloc_psum_tensor("bank%d" % i, [128, 512], F32)))

    def bank(self):
        b = self.banks[self.bank_i]
        self.bank_i = (self.bank_i + 1) % len(self.banks)
        return b

    @staticmethod
    def _norm(x):
        return (x, None) if isinstance(x, Res) else x

    def _collect(self, reads, writes):
        evs = {}

        def add(k, v):
            if evs.get(k, -1) < v:
                evs[k] = v

        for x in reads:
            res, sub = self._norm(x)
            subs = list(res.w.keys()) if sub is None else (sub, None)
            for s in subs:
                ev = res.w.get(s)
                if ev is not None:
                    add(*ev)
        for x in writes:
            res, sub = self._norm(x)
            subs = (set(res.w.keys()) | set(res.r.keys())) if sub is None else (sub, None)
            for s in subs:
                ev = res.w.get(s)
                if ev is not None:
                    add(*ev)
                rr = res.r.get(s)
                if rr:
                    for k, v in rr.items():
                        add(k, v)
        return evs

    def _record(self, reads, writes, ev):
        k, v = ev
        for x in reads:
            res, sub = self._norm(x)
            d = res.r.setdefault(sub, {})
            if d.get(k, -1) < v:
                d[k] = v
        for x in writes:
            res, sub = self._norm(x)
            if sub is None:
                res.w = {None: ev}
                res.r = {}
            else:
                res.w[sub] = ev
                res.r[sub] = {}

    def _waits(self, eng, evs):
        out = []
        for k, v in evs.items():
            if k == ("c", eng) and (eng in ("pe", "sp") or not SAME_ENGINE_SYNC):
                continue
            if self.waited.get((eng, k), -1) >= v:
                continue
            self.waited[(eng, k)] = v
            out.append((k, v))
        return out

    def _semof(self, k):
        return self.sem[k[1]] if k[0] == "c" else self.dsem[k[1]]

    def op(self, eng, fn, reads=(), writes=()):
        evs = self._collect(reads, writes)
        waits = self._waits(eng, evs)
        self.cnt[eng] += 1
        self._record(reads, writes, (("c", eng), self.cnt[eng]))
        self.ops[eng].append((waits, fn, self.sem[eng], 1))
        self.n_inst += 1

    def dma(self, out_ap, in_ap, reads=(), writes=(), queue="sp", **kw):
        evs = self._collect(reads, writes)
        lst = self.dq[queue]
        i = lst[self.dnext[queue] % len(lst)]
        self.dnext[queue] += 1
        if self.dval[i] > 0:
            k = ("d", i)
            if evs.get(k, -1) < self.dval[i]:
                evs[k] = self.dval[i]
        waits = self._waits(queue, evs)
        self.dval[i] += 16
        self._record(reads, writes, (("d", i), self.dval[i]))

        def fn(e, out_ap=out_ap, in_ap=in_ap, kw=kw):
            return e.dma_start(out=out_ap, in_=in_ap, **kw)
        self.ops[queue].append((waits, fn, self.dsem[i], 16))
        self.n_inst += 1

    def _all_events(self):
        evs = {("c", e): self.cnt[e] for e in self.ENGS if self.cnt[e] > 0}
        for i, v in enumerate(self.dval):
            if v > 0:
                evs[("d", i)] = v
        return evs

    def barrier(self):
        evs = self._all_events()
        for e in self.ENGS:
            w = []
            for k, v in evs.items():
                if k == ("c", e):
                    continue
                if self.waited.get((e, k), -1) >= v:
                    continue
                self.waited[(e, k)] = v
                w.append((k, v))
            if w:
                self.ops[e].append((w, None, None, 0))

    def final_wait(self, eng="sp"):
        evs = self._all_events()
        w = [(k, v) for k, v in evs.items() if k != ("c", eng)]
        self.ops[eng].append((w, None, None, 0))

    def emit(self):
        hmap = {"pe": "tensor", "act": "scalar", "dve": "vector", "pool": "gpsimd", "sp": "sync"}
        with self.nc.Block() as block:
            for e in self.ENGS:
                lst = self.ops[e]
                if not lst:
                    continue

                def body(h, lst=lst):
                    for waits, fn, sem, inc in lst:
                        for k, v in waits:
                            h.wait_ge(self._semof(k), v)
                        if fn is not None:
                            fn(h).then_inc(sem, inc)
                getattr(block, hmap[e])(body)

    def mm(self, out_ap, lhsT, rhs, start, stop, reads, writes):
        self.op("pe", lambda e: e.matmul(out_ap, lhsT=lhsT, rhs=rhs, start=start, stop=stop), reads, writes)

    def tr(self, out_ap, in_ap, ident_ap, reads, writes):
        self.op("pe", lambda e: e.transpose(out_ap, in_ap, ident_ap), reads, writes)

    def act(self, out_ap, in_ap, func, reads, writes, scale=None, bias=None):
        kw = {}
        if scale is not None:
            kw["scale"] = scale
        if bias is not None:
            kw["bias"] = bias
        self.op("act", lambda e: e.activation(out=out_ap, in_=in_ap, func=func, **kw), reads, writes)

    def tt(self, eng, out_ap, in0, in1, op, reads, writes):
        self.op(eng, lambda e: e.tensor_tensor(out=out_ap, in0=in0, in1=in1, op=op), reads, writes)

    def ts(self, eng, out_ap, in0, s1, s2, op0, op1, reads, writes):
        if op1 is None:
            self.op(eng, lambda e: e.tensor_scalar(out=out_ap, in0=in0, scalar1=s1, scalar2=None, op0=op0), reads, writes)
        else:
            self.op(eng, lambda e: e.tensor_scalar(out=out_ap, in0=in0, scalar1=s1, scalar2=s2, op0=op0, op1=op1), reads, writes)

    def stt(self, out_ap, in0, scalar, in1, op0, op1, reads, writes):
        self.op("dve", lambda e: e.scalar_tensor_tensor(out=out_ap, in0=in0, scalar=scalar, in1=in1, op0=op0, op1=op1),
                reads, writes)

    def cp(self, eng, out_ap, in_ap, reads, writes):
        if eng == "act":
            self.op("act", lambda e: e.activation(out=out_ap, in_=in_ap, func=AF.Copy), reads, writes)
        else:
            self.op(eng, lambda e: e.tensor_copy(out=out_ap, in_=in_ap), reads, writes)

    def memset(self, eng, ap, val, writes):
        self.op(eng, lambda e: e.memset(ap, val), (), writes)

    def red(self, out_ap, in_ap, op, reads, writes):
        self.op("dve", lambda e: e.tensor_reduce(out=out_ap, in_=in_ap, axis=AX.X, op=op), reads, writes)

    def recip(self, out_ap, in_ap, reads, writes):
        self.op("dve", lambda e: e.reciprocal(out=out_ap, in_=in_ap), reads, writes)


def _consts():
    c = {}
    i = np.arange(128)
    c["ident"] = np.eye(128, dtype=np.float32)
    perm = np.zeros((128, 128), np.float32)
    for m in range(128):
        partner = m + 32 if (m % 64) < 32 else m - 32
        perm[partner, m] = 1.0
    c["perm"] = perm
    blk = np.zeros((128, 128), np.float32)
    blk[:64, :64] = 1.0 / 64
    blk[64:, 64:] = 1.0 / 64
    c["blk64"] = blk
    t = np.arange(T_LAT)
    row = (t // 64).astype(np.float32)
    col = (t % 64).astype(np.float32)
    half = 32
    inv = (10000.0 ** (-np.arange(0, half, 2, dtype=np.float32) / half)).astype(np.float32)
    ang = np.concatenate([row[:, None] * inv, col[:, None] * inv], axis=-1).astype(np.float32)
    cos = np.cos(ang).astype(np.float32)
    sin = np.sin(ang).astype(np.float32)
    cosT = np.zeros((128, T_LAT), np.float32)
    sinsT = np.zeros((128, T_LAT), np.float32)
    for p in range(128):
        dd = p % 64
        cosT[p] = cos[:, dd % 32]
        sinsT[p] = sin[:, dd % 32] * (-1.0 if dd < 32 else 1.0)
    c["cosT"] = cosT
    c["sinsT"] = sinsT
    c["swa_prev"] = (i[:, None] >= i[None, :]).astype(np.float32)
    c["swa_next"] = (i[:, None] <= i[None, :]).astype(np.float32)
    dn = np.zeros((2, 3 + 7, 128, 128), np.float32)
    for d in range(2):
        if d == 0:
            tri = (i[:, None] <= i[None, :])
            sm = (i[:, None] > i[None, :])
            att = (i[None, :] >= i[:, None])
        else:
            tri = (i[:, None] >= i[None, :])
            sm = (i[:, None] < i[None, :])
            att = (i[None, :] <= i[:, None])
        dn[d, 0] = tri
        dn[d, 1] = sm
        dn[d, 2] = np.where(att, 0.0, NEG)
        for li in range(7):
            s = 1 << li
            same2 = (i[:, None] // (2 * s)) == (i[None, :] // (2 * s))
            diff1 = (i[:, None] // s) != (i[None, :] // s)
            strict = att & (i[:, None] != i[None, :])
            dn[d, 3 + li] = (same2 & diff1 & strict)
    c["dn_masks"] = dn
    CH = 32
    l64 = np.arange(CH)
    hg_a = np.zeros((2, CH, 2 * CH), np.float32)
    hg_last = np.zeros((2, CH, CH), np.float32)
    hg_mask = np.zeros((2, CH, CH), np.float32)
    for d in range(2):
        if d == 0:
            tri = (l64[:, None] <= l64[None, :]).astype(np.float32)
            mid = CH // 2 - 1
        else:
            tri = (l64[:, None] >= l64[None, :]).astype(np.float32)
            mid = CH // 2
        hg_a[d, :, 0:CH] = tri - tri[:, mid:mid + 1]
        hg_a[d, :, CH:2 * CH] = tri
        hg_last[d] = 1.0 - tri
        hg_mask[d] = (l64[:, None] <= l64[None, :]) if d == 0 else (l64[:, None] >= l64[None, :])
    c["hg_a"] = hg_a
    c["hg_last"] = hg_last
    c["hg_mask"] = hg_mask
    c["ones64"] = np.full((64, 64), 1.0 / 64, np.float32)
    return c


IN_OFF = dict(dn_qkv=0, dn_gate=768, dn_a=1024, dn_b=1032, sw_q=1040, sw_k=1552, sw_v=1680,
              hg_q=1808, hg_f=2064, hg_i=2576, hg_gate=2832)


def _prep_shared(inp):
    f = lambda a: np.ascontiguousarray(a, dtype=np.float32)
    s = {}
    w_in = inp["w_in"]
    o = IN_OFF
    s["w_dn"] = f(np.concatenate([w_in[:, :, 0:768], w_in[:, :, 768:1024], w_in[:, :, 1024:1040]], axis=2))
    s["w_sw"] = f(w_in[:, :, o["sw_q"]:o["sw_q"] + 768])
    s["w_hg"] = f(w_in[:, :, o["hg_q"]:o["hg_q"] + 1280])
    s["w_ada"] = f(inp["w_ada"])
    s["w_out"] = f(inp["w_out"])
    s["w_ff1"] = f(inp["w_ff1"])
    s["w_ff2"] = f(inp["w_ff2"])
    fm = lambda v: f(v.reshape(-1, 128).T)
    s["b_ada"] = f(np.stack([fm(inp["b_ada"][l]) for l in range(DEPTH)], axis=1))
    s["norm1"] = f(np.stack([fm(inp["norm1"][l]) for l in range(DEPTH)], axis=1))
    s["norm2"] = f(np.stack([fm(inp["norm2"][l]) for l in range(DEPTH)], axis=1))
    s["norm_f"] = fm(inp["norm_f"])
    s["dn_conv"] = f(np.transpose(inp["dn_conv"].reshape(DEPTH, 5, 6, 128), (3, 0, 2, 1)))
    rep = lambda v: f(np.broadcast_to(v.reshape(1, -1), (128, v.size)))
    s["dn_alog"] = rep(inp["dn_A_log"])
    s["dn_dtb"] = rep(inp["dn_dt_bias"])
    s["sink"] = rep(inp["swa_sink"])
    tile2 = lambda v: f(np.concatenate([v, v], axis=1).T)
    s["dn_norm"] = tile2(inp["dn_norm"])
    s["hg_norm"] = tile2(inp["hg_norm"])
    s["lb_rep"] = rep(inp["hg_lb_logits"])
    lbf = inp["hg_lb_logits"].reshape(2, DEPTH, 4, 64)
    s["lb_fm"] = f(np.transpose(lbf, (3, 0, 1, 2)))
    s.update(_consts())
    return s


ECLAMP = 2.35e17


def r_(ap):
    return ap.bitcast(F32R)


def build_program(n_layers=DEPTH, dbg=False, groups=("swa", "hg", "dn"), WL=DEPTH, stop=None):
    nc = bass.Bass("TRN2", target_bir_lowering=False)
    P = Prog(nc)
    P.init_banks(8)
    L = n_layers
    last_global = (n_layers == DEPTH)

    def din(name, shape):
        return P.dram(name, shape, F32, kind="ExternalInput")

    x_in = din("x", [2, T_LAT, D])
    ctx_in = din("ctx", [2, T_CTX, D])
    cT_in = din("cT", [128, 8, 3])
    w_ada = din("w_ada", [WL, D, 6 * D])
    b_ada = din("b_ada", [128, DEPTH, 48])
    norm1_in = din("norm1", [128, DEPTH, 8])
    norm2_in = din("norm2", [128, DEPTH, 8])
    normf_in = din("norm_f", [128, 8])
    w_dn = din("w_dn", [WL, D, 1040])
    w_sw = din("w_sw", [WL, D, 768])
    w_hg = din("w_hg", [WL, D, 1280])
    w_out = din("w_out", [WL, D, D])
    w_ff1 = din("w_ff1", [WL, D, 4 * D])
    w_ff2 = din("w_ff2", [WL, 4 * D, D])
    dn_conv_in = din("dn_conv", [128, DEPTH, 6, 5])
    dn_alog_in = din("dn_alog", [128, DEPTH * 8])
    dn_dtb_in = din("dn_dtb", [128, DEPTH * 8])
    sink_in = din("sink", [128, DEPTH * 8])
    dn_norm_in = din("dn_norm", [128, DEPTH])
    hg_norm_in = din("hg_norm", [128, DEPTH])
    lb_rep_in = din("lb_rep", [128, 2 * DEPTH * 256])
    lb_fm_in = din("lb_fm", [64, 2, DEPTH, 4])
    c_ident = din("ident", [128, 128])
    c_perm = din("perm", [128, 128])
    c_blk64 = din("blk64", [128, 128])
    c_cosT = din("cosT", [128, T_LAT])
    c_sinsT = din("sinsT", [128, T_LAT])
    c_swa_prev = din("swa_prev", [128, 128])
    c_swa_next = din("swa_next", [128, 128])
    c_dn_masks = din("dn_masks", [2, 10, 128, 128])
    c_hg_a = din("hg_a", [2, 32, 64])
    c_hg_last = din("hg_last", [2, 32, 32])
    c_hg_mask = din("hg_mask", [2, 32, 32])
    c_ones64 = din("ones64", [64, 64])

    out_d = P.dram("out", [2, T_LAT, D], F32, kind="ExternalOutput")
    XT = P.dram("XT", [2, 128, 8, TT], F32)
    QKV = P.dram("QKVs", [TT, 768], F32)
    LBS = P.dram("LBS", [2, 128, 2, DEPTH, 256], F32)
    HGS = P.dram("HGS", [TT, 2, 3, 256], F32)
    OS = P.dram("OS", [2, 64, 4, TT], F32)
    dbg_out = {}

    def dbg_tensor(name, shape):
        r = P.dram("dbg_" + name, shape, F32, kind="ExternalOutput")
        dbg_out[name] = r
        return r

    def xk(b, c0, n):
        return [(XT, (b, ti)) for ti in range(c0 // 128, (c0 + n + 127) // 128)]

    ident = P.sb("ident", [128, 128], F32)
    ones_bf = P.sb("ones_bf", [128, 128], BF16)
    ones_f = P.sb("ones_f", [128, 128], F32)
    blk64 = P.sb("blk64", [128, 128], BF16)
    cT = P.sb("cT", [128, 8, 3], F32)
    scT = P.sb("scT", [128, 8, 3], F32)
    modv = P.sb("modv", [128, 6, 8, 3], F32)
    g1 = P.sb("g1", [128, 8, 3], F32)
    g2 = P.sb("g2", [128, 8, 3], F32)
    bada = P.sb("bada", [128, DEPTH, 48], F32)
    n1 = P.sb("n1", [128, DEPTH, 8], F32)
    n2 = P.sb("n2", [128, DEPTH, 8], F32)
    nf = P.sb("nf", [128, 8], F32)
    dnnorm = P.sb("dnnorm", [128, DEPTH], F32)
    hgnorm = P.sb("hgnorm", [128, DEPTH], F32)
    epsb = P.sb("epsb", [128, 1], F32)
    oneb = P.sb("oneb", [128, 1], F32)
    zero_f = P.sb("zero_f", [128, 128], F32)
    lbfm = P.sb("lbfm", [64, 2, DEPTH, 4], F32)
    oml_fm = P.sb("oml_fm", [64, 2, DEPTH, 4], F32)
    ones64 = P.sb("ones64", [64, 64], BF16)
    zero256 = P.sb("zero256", [128, 256], F32)
    hT = None
    yT = None

    for dst, src in ((ident, c_ident), (cT, cT_in), (bada, b_ada), (n1, norm1_in), (n2, norm2_in), (nf, normf_in),
                     (dnnorm, dn_norm_in), (hgnorm, hg_norm_in)):
        P.dma(dst[:], src[:], reads=[src], writes=[dst])
    P.memset("pool", ones_bf[:], 1.0, [ones_bf])
    P.memset("pool", ones_f[:], 1.0, [ones_f])
    P.memset("pool", epsb[:], EPS, [epsb])
    P.memset("pool", oneb[:], 1.0, [oneb])
    P.memset("pool", zero_f[:], 0.0, [zero_f])
    P.memset("pool", zero256[:], 0.0, [zero256])
    P.dma(ones64[:], c_ones64[:], reads=[c_ones64], writes=[ones64], queue="pool")
    P.dma(blk64[:], c_blk64[:], reads=[c_blk64], writes=[blk64], queue="pool")
    P.act(scT[:], cT[:], AF.Silu, [cT], [scT])

    def rstd_from_sumsq(ps_ap, out_ap, scale, ps_res, out_res, npart=128):
        P.act(out_ap, ps_ap, AF.Ln, [ps_res, epsb], [out_res], scale=scale, bias=epsb[0:npart, 0:1])
        P.act(out_ap, out_ap, AF.Exp, [out_res], [out_res], scale=-0.5)

    def blocks512():
        yield (0, 256, 2)
        for k in range(4):
            yield (256 + 512 * k, 512, None)

    def load_w(dst, src_ap, n_k, src_res):
        for kc in range(n_k):
            P.dma(dst[:, kc, :], src_ap[kc * 128:(kc + 1) * 128, :], reads=[src_res], writes=[(dst, kc)], queue="pool")

    def feat_proj(wg, col0, c0, n):
        ps = P.bank()
        for kc in range(8):
            P.mm(ps[:, :n], wg[:, kc, col0:col0 + 128], hT[:, kc, c0:c0 + n], kc == 0, kc == 7, [wg, hT], [ps])
        return ps

    def tok_proj(ps_ap, ps, wg, col0, ncol, ti):
        for kc in range(8):
            P.mm(ps_ap, hT[:, kc, ti * 128:(ti + 1) * 128], wg[:, kc, col0:col0 + ncol], kc == 0, kc == 7, [wg, hT], [ps])

    def chain_steps(stepfn, d, n):
        for step in range(n):
            yield from stepfn(d, step)
            yield

    def run_streams(streams, lag):
        live = list(streams)
        tick = 0
        started = 1
        while live:
            for i, g in enumerate(list(live[:started])):
                try:
                    next(g)
                except StopIteration:
                    live.remove(g)
                    started -= 1
            tick += 1
            if started < len(live) and tick >= lag * started:
                started += 1

    def scan_order(d):
        return list(range(NT)) if d == 0 else [1, 0] + list(range(NT - 1, 1, -1))

    def head_out(l, need_ctx, wg, gcol0, gain, ych0):
        sqs = [P.sb("hn_sq", [64, 512], BF16) for _ in range(2)]
        rss = [P.sb("hn_rs", [64, 512], F32) for _ in range(2)]
        sgs = [P.sb("hn_sg", [64, 512], BF16) for _ in range(2)]
        yts = [P.sb("hn_yt", [64, 512], BF16) for _ in range(2)]
        o0s = [P.sb("hn_o0", [64, 512], F32) for _ in range(2)]
        o1s = [P.sb("hn_o1", [64, 512], F32) for _ in range(2)]
        k = 0
        for (c0, n, r) in blocks512():
            if r == 2 and not need_ctx:
                continue
            for h in range(4):
                sq, rs, sgt, yt, o0, o1 = sqs[k % 2], rss[k % 2], sgs[k % 2], yts[k % 2], o0s[k % 2], o1s[k % 2]
                k += 1
                okeys = lambda dd: [(OS, (dd, t)) for t in range(c0 // 128, (c0 + n) // 128)]
                P.dma(o0[:, :n], OS[0, :, h, c0:c0 + n], reads=okeys(0), writes=[o0])
                P.dma(o1[:, :n], OS[1, :, h, c0:c0 + n], reads=okeys(1), writes=[o1])
                P.tt("pool", o0[:, :n], o0[:, :n], o1[:, :n], ALU.add, [o0, o1], [o0])
                P.act(sq[:, :n], o0[:, :n], AF.Square, [o0], [sq])
                ps = P.bank()
                P.mm(ps[0:64, :n], ones64[:], sq[:, :n], True, True, [ones64, sq], [ps])
                rstd_from_sumsq(ps[0:64, :n], rs[:, :n], 1.0, ps, rs, 64)
                ps2 = P.bank()
                for kc in range(8):
                    P.mm(ps2[0:64, :n], wg[:, kc, gcol0 + h * 64:gcol0 + (h + 1) * 64], hT[:, kc, c0:c0 + n], kc == 0, kc == 7,
                         [wg, hT], [ps2])
                P.act(sgt[:, :n], ps2[0:64, :n], AF.Silu, [ps2], [sgt])
                P.tt("dve", rs[:, :n], o0[:, :n], rs[:, :n], ALU.mult, [o0, rs], [rs])
                if h % 2 == 0:
                    P.stt(yT[0:64, ych0 + h // 2, c0:c0 + n], rs[:, :n], gain[0:64, l:l + 1], sgt[:, :n], ALU.mult, ALU.mult,
                          [rs, gain, sgt], [yT])
                else:
                    P.stt(yt[:, :n], rs[:, :n], gain[0:64, l:l + 1], sgt[:, :n], ALU.mult, ALU.mult, [rs, gain, sgt], [yt])
                    P.dma(yT[64:128, ych0 + h // 2, c0:c0 + n], yt[:, :n], reads=[yt], writes=[yT])

    def swa_group(l, b, need_ctx):
        with P.scope():
            wg = P.sb("wsw", [128, 8, 768], BF16)
            load_w(wg, w_sw[l], 8, w_sw)
            qT = P.sb("sqT", [64, 8, TT], BF16)
            kT = P.sb("skT", [64, 2, TT], BF16)
            vaug = P.sb("vaug", [128, NT, 2, 72], BF16)
            cosT = P.sb("cosT", [64, T_LAT], F32)
            sinsT = P.sb("sinsT", [64, T_LAT], F32)
            perm = P.sb("perm", [64, 64], F32)
            mprev = P.sb("mprev", [128, 128], BF16)
            mnext = P.sb("mnext", [128, 128], BF16)
            esink = P.sb("esink", [128, 8], F32)
            P.dma(cosT[:], c_cosT[0:64, :], reads=[c_cosT], writes=[cosT])
            P.dma(sinsT[:], c_sinsT[0:64, :], reads=[c_sinsT], writes=[sinsT])
            perm_raw = P.sb("perm_raw", [64, 64], F32)
            P.dma(perm_raw[:], c_perm[0:64, 0:64], reads=[c_perm], writes=[perm_raw])
            P.cp("dve", r_(perm[:]), perm_raw[:], [perm_raw], [perm])
            P.dma(mprev[:], c_swa_prev[:], reads=[c_swa_prev], writes=[mprev], queue="pool")
            P.dma(mnext[:], c_swa_next[:], reads=[c_swa_next], writes=[mnext], queue="pool")
            P.dma(esink[:], sink_in[:, l * 8:(l + 1) * 8], reads=[sink_in], writes=[esink])
            P.act(esink[:], esink[:], AF.Exp, [esink], [esink])
            P.memset("pool", vaug[:].rearrange("p a b c -> p (a b c)"), 1.0, [vaug])
            qraw = [P.sb("qraw", [64, 512], F32) for _ in range(2)]
            t1s = [P.sb("rt1", [64, 512], F32) for _ in range(2)]
            t2s = [P.sb("rt2", [64, 512], F32) for _ in range(2)]
            k = 0
            for (c0, n, r) in blocks512():
                for hh in range(10):
                    ps = P.bank()
                    for kc in range(8):
                        P.mm(ps[0:64, :n], wg[:, kc, hh * 64:(hh + 1) * 64], hT[:, kc, c0:c0 + n], kc == 0, kc == 7, [wg, hT], [ps])
                    dst = qT[:, hh, c0:c0 + n] if hh < 8 else kT[:, hh - 8, c0:c0 + n]
                    dres = qT if hh < 8 else kT
                    if c0 < T_CTX:
                        P.cp("act", dst, ps[0:64, :n], [ps], [dres])
                    else:
                        tl = c0 - T_CTX
                        qr, t1, t2 = qraw[k % 2], t1s[k % 2], t2s[k % 2]
                        k += 1
                        P.cp("act", r_(qr[:, :n]), ps[0:64, :n], [ps], [qr])
                        ps2 = P.bank()
                        P.mm(ps2[0:64, :n], r_(perm[:]), r_(qr[:, :n]), True, True, [perm, qr], [ps2])
                        P.tt("pool", t1[:, :n], qr[:, :n], cosT[:, tl:tl + n], ALU.mult, [qr, cosT], [t1])
                        P.tt("dve", t2[:, :n], ps2[0:64, :n], sinsT[:, tl:tl + n], ALU.mult, [ps2, sinsT], [t2])
                        P.tt("dve", dst, t1[:, :n], t2[:, :n], ALU.add, [t1, t2], [dres])
            for ti in range(NT):
                ps = P.bank()
                tok_proj(ps[:, 0:128], ps, wg, 640, 128, ti)
                P.cp("act", vaug[:, ti, :, 0:64], ps[:, 0:128].rearrange("p (g d) -> p g d", g=2), [ps], [vaug])
            if stop == "swa_proj":
                return
            Es = [P.sb("E", [128, 512], BF16) for _ in range(10)]
            dens = [P.sb("den", [128, 4], F32) for _ in range(2)]
            ytoks = [P.sb("ytok", [128, 256], F32) for _ in range(2)]
            ei = 0
            gi = 0
            qtiles = ([0, 1] if need_ctx else []) + list(range(2, NT))
            for ti in qtiles:
                if ti < 2:
                    kblocks = [(0, None), (1, None)]
                else:
                    kblocks = []
                    if ti - 1 >= 2:
                        kblocks.append((ti - 1, mprev))
                    kblocks.append((ti, None))
                    if ti + 1 < NT:
                        kblocks.append((ti + 1, mnext))
                    kblocks += [(0, None), (1, None)]
                for g in range(2):
                    E = []
                    for (kb, mk) in kblocks:
                        ps = P.bank()
                        P.mm(ps[:].rearrange("p (j q) -> p j q", j=4), kT[:, g, kb * 128:(kb + 1) * 128],
                             qT[:, 4 * g:4 * g + 4, ti * 128:(ti + 1) * 128], True, True, [kT, qT], [ps])
                        e = Es[ei % len(Es)]
                        ei += 1
                        P.act(e[:], ps[:], AF.Exp, [ps], [e], scale=0.125)
                        if mk is not None:
                            e3 = e[:].rearrange("p (h q) -> p h q", h=4)
                            P.tt("dve", e3, e3, mk[:].unsqueeze(1).to_broadcast([128, 4, 128]), ALU.mult, [e, mk], [e])
                        E.append(e)
                    if stop == "swa_qk":
                        continue
                    pso = P.bank()
                    for hh in range(4):
                        for bi, (kb, mk) in enumerate(kblocks):
                            P.mm(pso[:, hh * 72:hh * 72 + 66], E[bi][:, hh * 128:(hh + 1) * 128], vaug[:, kb, g, 0:66],
                                 bi == 0, bi == len(kblocks) - 1, [E[bi], vaug], [pso])
                    if stop == "swa_pv":
                        continue
                    den, ytok = dens[gi % 2], ytoks[gi % 2]
                    gi += 1
                    po3 = pso[:, 0:288].rearrange("p (h e) -> p h e", h=4)
                    P.tt("dve", den[:], po3[:, :, 64], esink[:, 4 * g:4 * g + 4], ALU.add, [pso, esink], [den])
                    P.recip(den[:], den[:], [den], [den])
                    P.tt("dve", ytok[:].rearrange("p (h d) -> p h d", h=4), po3[:, :, 0:64],
                         den[:].unsqueeze(2).to_broadcast([128, 4, 64]), ALU.mult, [pso, den], [ytok])
                    pst = P.bank()
                    for j in range(2):
                        P.tr(pst[:, j * 128:(j + 1) * 128], ytok[:, j * 128:(j + 1) * 128], ident[:], [ytok, ident], [pst])
                    P.cp("act", yT[:, 2 + 2 * g:4 + 2 * g, ti * 128:(ti + 1) * 128],
                         pst[:, 0:256].rearrange("p (j q) -> p j q", j=2), [pst], [yT])

    def hg_group(l, b, need_ctx):
        CH = 32
        TS = 2 * CH
        NTS = TT // TS
        BLK = 256
        with P.scope():
            wg = P.sb("whg", [128, 8, 1280], BF16)
            load_w(wg, w_hg[l], 8, w_hg)
            hga = [P.sb("hga", [CH, 2 * CH], F32) for _ in range(2)]
            hglast = [P.sb("hglast", [CH, CH], F32) for _ in range(2)]
            hgmask = [P.sb("hgmask", [CH, CH], F32) for _ in range(2)]
            hga_raw = [P.sb("hga_raw", [CH, 2 * CH], F32) for _ in range(2)]
            hglast_raw = [P.sb("hglast_raw", [CH, CH], F32) for _ in range(2)]
            for d in range(2):
                P.dma(hga_raw[d][:], c_hg_a[d], reads=[c_hg_a], writes=[hga_raw[d]])
                P.dma(hglast_raw[d][:], c_hg_last[d], reads=[c_hg_last], writes=[hglast_raw[d]])
                P.dma(hgmask[d][:], c_hg_mask[d], reads=[c_hg_mask], writes=[hgmask[d]])
                P.cp("dve", r_(hga[d][:]), hga_raw[d][:], [hga_raw[d]], [hga[d]])
                P.cp("dve", r_(hglast[d][:]), hglast_raw[d][:], [hglast_raw[d]], [hglast[d]])
            S = [P.sb("hS", [64, 4, 64], F32) for _ in range(2)]
            for d in range(2):
                P.cp("dve", r_(S[d][:].rearrange("p a b -> p (a b)")), zero256[0:64, :], [zero256], [S[d]])
            with P.scope():
                lbr = P.sb("lbr", [128, 512], F32)
                omr = P.sb("omr", [128, 512], F32)
                P.dma(lbr[:].rearrange("p (d x) -> p d x", d=2), LBS[0, :, :, l, :], reads=[LBS], writes=[lbr])
                P.dma(omr[:].rearrange("p (d x) -> p d x", d=2), LBS[1, :, :, l, :], reads=[LBS], writes=[omr])
                sts = [P.sb("hst", [128, 2, 3, 256], F32) for _ in range(2)]
                s1s = [P.sb("hs1", [128, 512], F32) for _ in range(2)]
                s2s = [P.sb("hs2", [128, 512], F32) for _ in range(2)]
                for ti in range(NT):
                    st, s1, s2 = sts[ti % 2], s1s[ti % 2], s2s[ti % 2]
                    psz = P.bank()
                    tok_proj(psz[:, 0:512], psz, wg, 256, 512, ti)
                    psv = P.bank()
                    tok_proj(psv[:, 0:256], psv, wg, 768, 256, ti)
                    P.act(s1[:], psz[:], AF.Sigmoid, [psz], [s1])
                    P.act(s2[:], psz[:], AF.Sigmoid, [psz], [s2], scale=-1.0)
                    P.tt("dve", s1[:], s1[:], omr[:], ALU.mult, [s1, omr], [s1])
                    P.tt("dve", s1[:], s1[:], lbr[:], ALU.add, [s1, lbr], [s1])
                    P.act(st[:, :, 0, :], s1[:].rearrange("p (d x) -> p d x", d=2), AF.Ln, [s1], [(st, 0)])
                    P.tt("pool", st[:, :, 1, :], s2[:].rearrange("p (d x) -> p d x", d=2), omr[:].rearrange("p (d x) -> p d x", d=2),
                         ALU.mult, [s2, omr], [(st, 1)])
                    P.cp("act", st[:, 0, 2, :], psv[:, 0:256], [psv], [(st, 2)])
                    P.cp("dve", st[:, 1, 2, :], psv[:, 0:256], [psv], [(st, 3)])
                    P.dma(HGS[ti * 128:(ti + 1) * 128], st[:], reads=[st], writes=[(HGS, ti)])
            with P.scope():
                def mk(name, shape, n=2):
                    return [P.sb(name, shape, F32) for _ in range(n)]
                qblk, kblk = mk("hqblk", [64, 4, BLK]), mk("hkblk", [64, 4, BLK])
                ld, vtm, lfr = mk("hld", [CH, 2, 3, 256]), mk("hv", [CH, 2, 256]), mk("hlfr", [CH, 2, 256])
                E13, E2, E4 = mk("hE13", [64, 2, 4, 2 * CH]), mk("hE2", [64, 2, 4, CH]), mk("hE4", [CH, 2, 256])
                qq, kt, kh, PT, stmp = mk("hqq", [64, 2, 4, 2, CH]), mk("hkt", [64, 2, 4, CH]), mk("hkh", [CH, 2, 256]), \
                    mk("hPT", [CH, 2, 4, CH]), mk("hstmp", [64, 4, 64])
                nctx = T_CTX // TS
                orders = [list(range(NTS)), list(range(nctx - 1, -1, -1)) + list(range(NTS - 1, nctx - 1, -1))]
                cur_blk = [None, None]

                def load_block(d, blk):
                    c0 = blk * BLK
                    fo = 256 + d * 256
                    for h in range(4):
                        ps = P.bank()
                        for kc in range(8):
                            P.mm(ps[0:64, 0:BLK], wg[:, kc, h * 64:(h + 1) * 64], hT[:, kc, c0:c0 + BLK], kc == 0, kc == 7, [wg, hT], [ps])
                        P.cp("act", qblk[d][:, h, :], ps[0:64, 0:BLK], [ps], [qblk[d]])
                        ps = P.bank()
                        for kc in range(8):
                            P.mm(ps[0:64, 0:BLK], wg[:, kc, fo + h * 64:fo + (h + 1) * 64], hT[:, kc, c0:c0 + BLK], kc == 0, kc == 7,
                                 [wg, hT], [ps])
                        P.act(kblk[d][:, h, :], ps[0:64, 0:BLK], AF.Sigmoid, [ps], [kblk[d]], scale=-1.0)
                    P.tt("dve", kblk[d][:], kblk[d][:], oml_fm[:, d, l, :].unsqueeze(2).to_broadcast([64, 4, BLK]), ALU.mult,
                         [kblk[d], oml_fm], [kblk[d]])

                ots = mk("hot", [64, 4, TS], 4)
                otc = [0]

                def hg_step(d, step):
                        ti = orders[d][step]
                        tok = slice(ti * TS, (ti + 1) * TS)
                        blk = (ti * TS) // BLK
                        off = (ti * TS) % BLK
                        if cur_blk[d] != blk:
                            load_block(d, blk)
                            cur_blk[d] = blk
                        Sd = S[d]
                        i_last = (CH - 1) if d == 0 else 0
                        P.dma(ld[d][:], HGS[ti * TS:(ti + 1) * TS, d].rearrange("(c p) k x -> p c k x", p=CH),
                              reads=[(HGS, (ti * TS) // 128)], writes=[ld[d]])
                        P.cp("pool", r_(vtm[d][:]), ld[d][:, :, 2, :], [ld[d]], [vtm[d]])
                        P.cp("pool", r_(lfr[d][:]), ld[d][:, :, 0, :], [ld[d]], [lfr[d]])
                        yield
                        psA = P.bank()
                        psK = P.bank()
                        for c in range(2):
                            for h in range(4):
                                P.mm(psA[0:64, (c * 4 + h) * 2 * CH:(c * 4 + h + 1) * 2 * CH], r_(lfr[d][:, c, h * 64:(h + 1) * 64]), r_(hga[d][:]),
                                     True, True, [lfr[d], hga[d]], [psA])
                            P.mm(psK[0:CH, c * 256:(c + 1) * 256], r_(hglast[d][:]), r_(lfr[d][:, c, :]), True, True, [hglast[d], lfr[d]], [psK])
                        P.act(E13[d][:].rearrange("p c h x -> p (c h x)"), psA[0:64, 0:16 * CH], AF.Exp, [psA], [E13[d]])
                        P.act(E2[d][:].rearrange("p c h x -> p (c h) x"),
                              psA[0:64, 0:16 * CH].rearrange("p (a x) -> p a x", a=8)[:, :, 0:CH], AF.Exp, [psA], [E2[d]], scale=-1.0)
                        P.act(E4[d][:].rearrange("p c x -> p (c x)"), psK[0:CH, :], AF.Exp, [psK], [E4[d]])
                        yield
                        for c in range(2):
                            cs = slice(off + c * CH, off + (c + 1) * CH)
                            P.tt("dve", r_(qq[d][:, c, :, 0, :]), qblk[d][:, :, cs], E13[d][:, c, :, 0:CH], ALU.mult, [qblk[d], E13[d]], [qq[d]])
                            P.tt("dve", r_(qq[d][:, c, :, 1, :]), qblk[d][:, :, cs], E13[d][:, c, :, CH:2 * CH], ALU.mult, [qblk[d], E13[d]], [qq[d]])
                            P.tt("pool", r_(kt[d][:, c, :, :]), kblk[d][:, :, cs], E2[d][:, c, :, :], ALU.mult, [kblk[d], E2[d]], [kt[d]])
                        P.tt("pool", r_(kh[d][:]), ld[d][:, :, 1, :], E4[d][:], ALU.mult, [ld[d], E4[d]], [kh[d]])
                        yield
                        psS = P.bank()
                        for c in range(2):
                            for h in range(4):
                                P.mm(psS[0:CH, (c * 4 + h) * CH:(c * 4 + h + 1) * CH], r_(kt[d][:, c, h, :]), r_(qq[d][:, c, h, 0, :]),
                                     True, True, [kt[d], qq[d]], [psS])
                        P.tt("dve", r_(PT[d][:].rearrange("p c h i -> p (c h) i")), psS[0:CH, 0:8 * CH].rearrange("p (a i) -> p a i", a=8),
                             hgmask[d][:].unsqueeze(1).to_broadcast([CH, 8, CH]), ALU.mult, [psS, hgmask[d]], [PT[d]])
                        yield
                        pso = P.bank()
                        for c in ([0, 1] if d == 0 else [1, 0]):
                            for h in range(4):
                                o_ap = pso[0:64, h * TS + c * CH:h * TS + (c + 1) * CH]
                                P.mm(o_ap, r_(vtm[d][:, c, h * 64:(h + 1) * 64]), r_(PT[d][:, c, h, :]), True, False, [vtm[d], PT[d]], [pso])
                                P.mm(o_ap, r_(Sd[:, h, :]), r_(qq[d][:, c, h, 1, :]), False, True, [Sd, qq[d]], [pso])
                            psU = P.bank()
                            for h in range(4):
                                P.mm(psU[0:64, h * 64:(h + 1) * 64], r_(kh[d][:, c, h * 64:(h + 1) * 64]), r_(vtm[d][:, c, h * 64:(h + 1) * 64]),
                                     True, True, [kh[d], vtm[d]], [psU])
                            P.tt("dve", stmp[d][:], Sd[:], E13[d][:, c, :, CH + i_last].unsqueeze(2).to_broadcast([64, 4, 64]), ALU.mult,
                                 [Sd, E13[d]], [stmp[d]])
                            P.tt("dve", r_(Sd[:]), stmp[d][:], psU[0:64, 0:256].rearrange("p (h x) -> p h x", h=4), ALU.add,
                                 [stmp[d], psU], [Sd])
                        ot = ots[otc[0] % 4]
                        otc[0] += 1
                        P.cp("act", ot[:].rearrange("p h i -> p (h i)"), pso[0:64, 0:4 * TS], [pso], [ot])
                        P.dma(OS[d, :, :, tok], ot[:], reads=[ot], writes=[(OS, (d, (ti * TS) // 128))])

                run_streams([chain_steps(hg_step, 0, NTS), chain_steps(hg_step, 1, NTS)], lag=4)
            head_out(l, need_ctx, wg, 1024, hgnorm, 6)

    def dn_group(l, b, need_ctx):
        with P.scope():
            wsm = P.sb("wdn_s", [128, 8, 272], BF16)
            for kc in range(8):
                P.dma(wsm[:, kc, :], w_dn[l][kc * 128:(kc + 1) * 128, 768:1040], reads=[w_dn], writes=[(wsm, kc)], queue="pool")
            with P.scope():
                wg = P.sb("wdn", [128, 8, 768], BF16)
                for kc in range(8):
                    P.dma(wg[:, kc, :], w_dn[l][kc * 128:(kc + 1) * 128, 0:768], reads=[w_dn], writes=[(wg, kc)], queue="pool")
                convw = P.sb("convw", [128, 6, 5], F32)
                P.dma(convw[:], dn_conv_in[:, l, :, :], reads=[dn_conv_in], writes=[convw])
                raws = [P.sb("raw", [128, TT], F32) for _ in range(2)]
                cvs = [P.sb("cv", [128, TT], F32) for _ in range(2)]
                stg = [P.sb("stg", [128, 4, 128], F32) for _ in range(2)]
                si = 0
                for ch in range(6):
                    rw, cv = raws[ch % 2], cvs[ch % 2]
                    for (c0, n, r) in blocks512():
                        ps = feat_proj(wg, ch * 128, c0, n)
                        P.cp("act", rw[:, c0:c0 + n], ps[:, :n], [ps], [rw])
                    for (lo, hi) in ((0, T_CTX), (T_CTX, TT)):
                        P.ts("dve", cv[:, lo:hi], rw[:, lo:hi], convw[:, ch, 2:3], None, ALU.mult, None, [rw, convw], [cv])
                        for s in (0, 1, 3, 4):
                            dl = s - 2
                            a, bb = max(lo, lo - dl), min(hi, hi - dl)
                            P.stt(cv[:, a:bb], rw[:, a + dl:bb + dl], convw[:, ch, s:s + 1], cv[:, a:bb], ALU.mult, ALU.add,
                                  [rw, convw, cv], [cv])
                    P.act(cv[:], cv[:], AF.Silu, [cv], [cv])
                    for t0 in range(0, NT, 4):
                        nt = min(4, NT - t0)
                        ps = P.bank()
                        for k in range(nt):
                            P.tr(ps[:, k * 128:(k + 1) * 128], cv[:, (t0 + k) * 128:(t0 + k + 1) * 128], ident[:], [cv, ident], [ps])
                        st = stg[si % 2]
                        si += 1
                        P.cp("act" if si % 2 else "dve", st[:, 0:nt, :], ps[:, 0:nt * 128].rearrange("p (k c) -> p k c", k=nt),
                             [ps], [st])
                        P.dma(QKV[t0 * 128:(t0 + nt) * 128, ch * 128:(ch + 1) * 128].rearrange("(k p) c -> p k c", p=128),
                              st[:, 0:nt, :], reads=[st], writes=[(QKV, (t0, ch))])
            with P.scope():
                dnm = [P.sb("dnm", [128, 3, 128], F32) for _ in range(2)]
                dnl = [P.sb("dnl", [128, 7, 128], BF16) for _ in range(2)]
                for d in range(2):
                    P.dma(dnm[d][:], c_dn_masks[d, 0:3].rearrange("m p c -> p m c"), reads=[c_dn_masks], writes=[dnm[d]])
                    P.dma(dnl[d][:], c_dn_masks[d, 3:10].rearrange("m p c -> p m c"), reads=[c_dn_masks], writes=[dnl[d]], queue="pool")
                negA = P.sb("negA", [128, 8], F32)
                dtb = P.sb("dtb", [128, 8], F32)
                P.dma(negA[:], dn_alog_in[:, l * 8:(l + 1) * 8], reads=[dn_alog_in], writes=[negA])
                P.dma(dtb[:], dn_dtb_in[:, l * 8:(l + 1) * 8], reads=[dn_dtb_in], writes=[dtb])
                P.act(negA[:], negA[:], AF.Exp, [negA], [negA])
                P.ts("dve", negA[:], negA[:], -1.0, None, ALU.mult, None, [negA], [negA])
                S = [P.sb("dS", [64, 4, 64], F32) for _ in range(2)]
                for d in range(2):
                    P.cp("dve", r_(S[d][:].rearrange("p a b -> p (a b)")), zero256[0:64, :], [zero256], [S[d]])

                def mk(name, shape, n=2):
                    return [P.sb(name, shape, F32) for _ in range(n)]
                qkvt = mk("dqkv", [128, 768], 2)
                sqt, ss, rs, xa, ax, sp, gg, beta = mk("dsq", [128, 512]), mk("dss", [128, 8]), mk("drs", [128, 8]), \
                    mk("dxa", [128, 8]), mk("dax", [128, 8]), mk("dsp", [128, 8]), mk("dg", [128, 8]), mk("dbeta", [128, 8])
                qn, kn, knT, qnT = mk("dqn", [128, 4, 64]), mk("dkn", [128, 4, 64]), mk("dknT", [64, 4, 128]), mk("dqnT", [64, 4, 128])
                gcs, egc, elast, etot = mk("dgcs", [128, 8]), mk("degc", [128, 4]), mk("delast", [128, 4]), mk("detot", [128, 4])
                gm, DMT, AT, scTt, bt, Zb = mk("dgm", [128, 4, 128], 2), mk("dDMT", [128, 512], 2), mk("dAT", [128, 512], 2), \
                    mk("dscT", [128, 512], 2), mk("dbt", [128, 512], 2), mk("dZ", [128, 512], 2)
                Tb2 = [mk("dT", [128, 512], 2) for _ in range(2)]
                TTb2 = [mk("dTT", [128, 512], 2) for _ in range(2)]
                ots = mk("dot", [64, 4, 128], 2)
                kb, kbg, vb, qh, khat = mk("dkb", [128, 4, 64]), mk("dkbg", [128, 4, 64]), mk("dvb", [128, 4, 64]), \
                    mk("dqh", [128, 4, 64]), mk("dkhat", [128, 4, 64])
                kbT, qhT, ub, wTb, vnb, stmp = mk("dkbT", [64, 4, 128]), mk("dqhT", [64, 4, 128]), mk("du", [128, 256]), \
                    mk("dwT", [64, 4, 128]), mk("dvn", [128, 256]), mk("dstmp", [64, 4, 64])
                identb4 = ident[:].unsqueeze(1).to_broadcast([128, 4, 128])

                def tr_heads(dst, src, src_res):
                    pst = P.bank()
                    for h in range(4):
                        P.tr(pst[0:64, h * 128:(h + 1) * 128], src[:, h, :], ident[:], [src_res, ident], [pst])
                    P.cp("act", r_(dst[:].rearrange("p h t -> p (h t)")), pst[0:64, :], [pst], [dst])

                def prep_tile(ti, slot):
                    tok = slice(ti * 128, (ti + 1) * 128)
                    s_ = slot % 2
                    qk = qkvt[s_]
                    P.dma(qk[:], QKV[tok, :], reads=[QKV], writes=[qk])
                    psab = P.bank()
                    for kc in range(8):
                        P.mm(psab[:, 0:16], hT[:, kc, tok], wsm[:, kc, 256:272], kc == 0, kc == 7, [wsm, hT], [psab])
                    P.tt("dve", xa[s_][:], psab[:, 0:8], dtb[:], ALU.add, [psab, dtb], [xa[s_]])
                    P.stt(ax[s_][:], xa[s_][:], -1.0, xa[s_][:], ALU.mult, ALU.max, [xa[s_]], [ax[s_]])
                    P.act(ax[s_][:], ax[s_][:], AF.Exp, [ax[s_]], [ax[s_]], scale=-1.0)
                    P.act(ax[s_][:], ax[s_][:], AF.Ln, [ax[s_], oneb], [ax[s_]], bias=oneb[:, 0:1])
                    P.stt(sp[s_][:], xa[s_][:], 0.0, ax[s_][:], ALU.max, ALU.add, [xa[s_], ax[s_]], [sp[s_]])
                    P.tt("dve", gg[s_][:], sp[s_][:], negA[:], ALU.mult, [sp[s_], negA], [gg[s_]])
                    P.act(beta[s_][:], psab[:, 8:16], AF.Exp, [psab], [beta[s_]], scale=-1.0)
                    P.ts("dve", beta[s_][:], beta[s_][:], 1.0, None, ALU.add, None, [beta[s_]], [beta[s_]])
                    P.recip(beta[s_][:], beta[s_][:], [beta[s_]], [beta[s_]])
                    P.tt("pool", sqt[s_][:], qk[:, 0:512], qk[:, 0:512], ALU.mult, [qk], [sqt[s_]])
                    P.red(ss[s_][:], sqt[s_][:].rearrange("p (h d) -> p h d", d=64), ALU.add, [sqt[s_]], [ss[s_]])
                    rstd_from_sumsq(ss[s_][:], rs[s_][:], 1.0, ss[s_], rs[s_])
                    P.ts("dve", rs[s_][:, 0:4], rs[s_][:, 0:4], 0.125, None, ALU.mult, None, [rs[s_]], [rs[s_]])
                    q3 = qk[:, 0:256].rearrange("p (h d) -> p h d", d=64)
                    k3 = qk[:, 256:512].rearrange("p (h d) -> p h d", d=64)
                    P.tt("dve", qn[s_][:], q3, rs[s_][:, 0:4].unsqueeze(2).to_broadcast([128, 4, 64]), ALU.mult, [qk, rs[s_]], [qn[s_]])
                    P.tt("dve", kn[s_][:], k3, rs[s_][:, 4:8].unsqueeze(2).to_broadcast([128, 4, 64]), ALU.mult, [qk, rs[s_]], [kn[s_]])
                    tr_heads(knT[s_], kn[s_], kn[s_])
                    tr_heads(qnT[s_], qn[s_], qn[s_])
                    return qk, s_

                orders = [scan_order(0), scan_order(1)]
                def dn_step(d, step):
                        ti = orders[d][step]
                        qk, s_ = prep_tile(ti, 2 * step + d)
                        Tb, TTb = Tb2[d], TTb2[d]
                        yield
                        tok = slice(ti * 128, (ti + 1) * 128)
                        m = dnm[d]
                        ml_ = dnl[d]
                        Sd = S[d]
                        g_d = gg[s_][:, d * 4:(d + 1) * 4]
                        b_d = beta[s_][:, d * 4:(d + 1) * 4]
                        psg = P.bank()
                        P.mm(psg[:, 0:4], m[:, 0, :], g_d, True, True, [m, gg[s_]], [psg])
                        P.mm(psg[:, 4:8], ones_f[:], g_d, True, True, [ones_f, gg[s_]], [psg])
                        P.cp("dve", gcs[d][:], psg[:, 0:8], [psg], [gcs[d]])
                        P.act(egc[d][:], gcs[d][:, 0:4], AF.Exp, [gcs[d]], [egc[d]])
                        P.tt("dve", elast[d][:], gcs[d][:, 4:8], gcs[d][:, 0:4], ALU.subtract, [gcs[d]], [elast[d]])
                        P.act(elast[d][:], elast[d][:], AF.Exp, [elast[d]], [elast[d]])
                        P.act(etot[d][:], gcs[d][:, 4:8], AF.Exp, [gcs[d]], [etot[d]])
                        P.tt("dve", gm[d][:], m[:, 0, :].unsqueeze(1).to_broadcast([128, 4, 128]),
                             g_d.unsqueeze(2).to_broadcast([128, 4, 128]), ALU.mult, [m, gg[s_]], [gm[d]])
                        psD = P.bank()
                        for h in range(4):
                            P.mm(psD[:, h * 128:(h + 1) * 128], m[:, 1, :], gm[d][:, h, :], True, False, [m, gm[d]], [psD])
                            P.mm(psD[:, h * 128:(h + 1) * 128], ident[:], m[:, 2, :], False, True, [ident, m], [psD])
                        P.act(DMT[d][:], psD[:], AF.Exp, [psD], [DMT[d]])
                        yield
                        v3 = qk[:, 512:768].rearrange("p (h d) -> p h d", d=64)
                        bb4 = b_d.unsqueeze(2).to_broadcast([128, 4, 64])
                        eg4 = egc[d][:].unsqueeze(2).to_broadcast([128, 4, 64])
                        el4 = elast[d][:].unsqueeze(2).to_broadcast([128, 4, 64])
                        P.tt("pool", kb[d][:], kn[s_][:], bb4, ALU.mult, [kn[s_], beta[s_]], [kb[d]])
                        P.tt("pool", r_(kbg[d][:]), kb[d][:], eg4, ALU.mult, [kb[d], egc[d]], [kbg[d]])
                        P.tt("pool", r_(vb[d][:]), v3, bb4, ALU.mult, [qk, beta[s_]], [vb[d]])
                        P.tt("pool", qh[d][:], qn[s_][:], eg4, ALU.mult, [qn[s_], egc[d]], [qh[d]])
                        P.tt("pool", r_(khat[d][:]), kn[s_][:], el4, ALU.mult, [kn[s_], elast[d]], [khat[d]])
                        tr_heads(kbT[d], kb[d], kb[d])
                        tr_heads(qhT[d], qh[d], qh[d])
                        yield
                        psG = P.bank()
                        psQ = P.bank()
                        for h in range(4):
                            P.mm(psG[:, h * 128:(h + 1) * 128], r_(knT[s_][:, h, :]), r_(kbT[d][:, h, :]), True, True,
                                 [knT[s_], kbT[d]], [psG])
                            P.mm(psQ[:, h * 128:(h + 1) * 128], r_(knT[s_][:, h, :]), r_(qnT[s_][:, h, :]), True, True,
                                 [knT[s_], qnT[s_]], [psQ])
                        P.tt("dve", AT[d][:], psG[:], DMT[d][:], ALU.mult, [psG, DMT[d]], [AT[d]])
                        P.tt("dve", r_(scTt[d][:]), psQ[:], DMT[d][:], ALU.mult, [psQ, DMT[d]], [scTt[d]])
                        yield
                        A3 = AT[d][:].rearrange("p (h i) -> p h i", h=4)
                        b3 = bt[d][:].rearrange("p (h i) -> p h i", h=4)

                        def lvl_mask(li):
                            return ml_[:, li, :].unsqueeze(1).to_broadcast([128, 4, 128])
                        cur, nxt = 0, 1
                        P.tt("pool", r_(b3), A3, lvl_mask(0), ALU.mult, [AT[d], ml_], [bt[d]])
                        P.tt("pool", r_(TTb[cur][:].rearrange("p (h i) -> p h i", h=4)), identb4, b3, ALU.subtract,
                             [ident, bt[d]], [TTb[cur]])
                        psT = P.bank()
                        for h in range(4):
                            P.tr(psT[:, h * 128:(h + 1) * 128], TTb[cur][:, h * 128:(h + 1) * 128], ident[:], [TTb[cur], ident], [psT])
                        P.cp("act", r_(Tb[cur][:]), psT[:], [psT], [Tb[cur]])
                        for li in range(1, 7):
                            P.tt("pool", r_(b3), A3, lvl_mask(li), ALU.mult, [AT[d], ml_], [bt[d]])
                            psZ = P.bank()
                            for h in range(4):
                                hs_ = slice(h * 128, (h + 1) * 128)
                                P.mm(psZ[:, hs_], r_(bt[d][:, hs_]), r_(Tb[cur][:, hs_]), True, True, [bt[d], Tb[cur]], [psZ])
                            P.cp("act", r_(Zb[d][:]), psZ[:], [psZ], [Zb[d]])
                            if li < 6:
                                psW = P.bank()
                                for h in range(4):
                                    hs_ = slice(h * 128, (h + 1) * 128)
                                    P.mm(psW[:, hs_], r_(TTb[cur][:, hs_]), r_(Zb[d][:, hs_]), True, True, [TTb[cur], Zb[d]], [psW])
                                P.tt("dve", r_(Tb[nxt][:]), Tb[cur][:], psW[:], ALU.subtract, [Tb[cur], psW], [Tb[nxt]])
                            psWT = P.bank()
                            for h in range(4):
                                hs_ = slice(h * 128, (h + 1) * 128)
                                P.mm(psWT[:, hs_], r_(Zb[d][:, hs_]), r_(TTb[cur][:, hs_]), True, True, [Zb[d], TTb[cur]], [psWT])
                            P.tt("dve", r_(TTb[nxt][:]), TTb[cur][:], psWT[:], ALU.subtract, [TTb[cur], psWT], [TTb[nxt]])
                            cur, nxt = nxt, cur
                            yield
                        TTf = TTb[cur]
                        psU = P.bank()
                        psWt = P.bank()
                        for h in range(4):
                            hs_ = slice(h * 128, (h + 1) * 128)
                            P.mm(psU[:, h * 64:(h + 1) * 64], r_(TTf[:, hs_]), r_(vb[d][:, h, :]), True, True, [TTf, vb[d]], [psU])
                            P.mm(psWt[0:64, hs_], r_(kbg[d][:, h, :]), r_(TTf[:, hs_]), True, True, [kbg[d], TTf], [psWt])
                        P.cp("act", ub[d][:], psU[:, 0:256], [psU], [ub[d]])
                        P.cp("act", r_(wTb[d][:].rearrange("p h t -> p (h t)")), psWt[0:64, :], [psWt], [wTb[d]])
                        yield
                        psV = P.bank()
                        for h in range(4):
                            P.mm(psV[:, h * 64:(h + 1) * 64], r_(wTb[d][:, h, :]), r_(Sd[:, h, :]), True, True, [wTb[d], Sd], [psV])
                        P.tt("dve", r_(vnb[d][:]), ub[d][:], psV[:, 0:256], ALU.subtract, [ub[d], psV], [vnb[d]])
                        psO = P.bank()
                        psS = P.bank()
                        for h in range(4):
                            o_ap = psO[0:64, h * 128:(h + 1) * 128]
                            P.mm(o_ap, r_(Sd[:, h, :]), r_(qhT[d][:, h, :]), True, False, [Sd, qhT[d]], [psO])
                            P.mm(o_ap, r_(vnb[d][:, h * 64:(h + 1) * 64]), r_(scTt[d][:, h * 128:(h + 1) * 128]), False, True,
                                 [vnb[d], scTt[d]], [psO])
                            P.mm(psS[0:64, h * 64:(h + 1) * 64], r_(khat[d][:, h, :]), r_(vnb[d][:, h * 64:(h + 1) * 64]), True, True,
                                 [khat[d], vnb[d]], [psS])
                        P.cp("act", ots[d][:].rearrange("p h i -> p (h i)"), psO[0:64, :], [psO], [ots[d]])
                        P.dma(OS[d, :, :, tok], ots[d][:], reads=[ots[d]], writes=[(OS, (d, ti))])
                        P.tt("dve", stmp[d][:], Sd[:], etot[d][0:64, :].unsqueeze(2).to_broadcast([64, 4, 64]), ALU.mult,
                             [Sd, etot[d]], [stmp[d]])
                        P.tt("dve", r_(Sd[:]), stmp[d][:], psS[0:64, 0:256].rearrange("p (h x) -> p h x", h=4), ALU.add,
                             [stmp[d], psS], [Sd])

                run_streams([chain_steps(dn_step, 0, NT), chain_steps(dn_step, 1, NT)], lag=7)
            head_out(l, need_ctx, wsm, 0, dnnorm, 0)

    with P.scope():
        xin = [P.sb("xin", [128, D], F32) for _ in range(2)]
        xo = [P.sb("xo", [128, 8, 128], F32) for _ in range(2)]
        k = 0
        for b in range(2):
            for ti in range(NT):
                src = ctx_in[b, ti * 128:(ti + 1) * 128, :] if ti < 2 else x_in[b, (ti - 2) * 128:(ti - 1) * 128, :]
                src_res = ctx_in if ti < 2 else x_in
                xi = xin[k % 2]
                xt = xo[k % 2]
                k += 1
                P.dma(xi[:], src, reads=[src_res], writes=[xi])
                for hb in range(2):
                    ps = P.bank()
                    for c4 in range(4):
                        c = hb * 4 + c4
                        P.tr(ps[:, c4 * 128:(c4 + 1) * 128], xi[:, c * 128:(c + 1) * 128], ident[:], [xi, ident], [ps])
                    P.cp("act" if hb else "dve", xt[:, hb * 4:hb * 4 + 4, :],
                         ps[:].rearrange("p (c t) -> p c t", c=4), [ps], [(xt, hb)])
                P.dma(XT[b, :, :, ti * 128:(ti + 1) * 128], xt[:], reads=[xt], writes=[(XT, (b, ti))])

    with P.scope():
        for (inner, src, NP) in ((256, lb_rep_in, 128), (4, lb_fm_in, 64)):
            raw = P.sb("lbraw", [NP, 2, DEPTH, inner], F32)
            lb = P.sb("lbt", [NP, 2, DEPTH, inner], F32)
            oml = P.sb("omlt", [NP, 2, DEPTH, inner], F32)
            mx = P.sb("lbmx", [NP, 2, inner], F32)
            sm = P.sb("lbsm", [NP, 2, inner], F32)
            srcap = src[:].rearrange("p (d l c) -> p d l c", d=2, l=DEPTH) if inner == 256 else src[:]
            P.dma(raw[:], srcap, reads=[src], writes=[raw])
            P.cp("dve", mx[:], raw[:, :, 0, :], [raw], [mx])
            for ll in range(1, DEPTH):
                P.tt("dve", mx[:], mx[:], raw[:, :, ll, :], ALU.max, [mx, raw], [mx])
            for ll in range(DEPTH):
                P.tt("dve", raw[:, :, ll, :], raw[:, :, ll, :], mx[:], ALU.subtract, [raw, mx], [raw])
            P.act(raw[:], raw[:], AF.Exp, [raw], [raw])
            P.cp("dve", sm[:], raw[:, :, 0, :], [raw], [sm])
            for ll in range(1, DEPTH):
                P.tt("dve", sm[:], sm[:], raw[:, :, ll, :], ALU.add, [sm, raw], [sm])
            P.recip(sm[:], sm[:], [sm], [sm])
            for ll in range(DEPTH):
                P.tt("dve", raw[:, :, ll, :], raw[:, :, ll, :], sm[:], ALU.mult, [raw, sm], [raw])
            P.memset("dve", lb[:, :, 0, :], 0.0, [lb])
            for ll in range(1, DEPTH):
                P.tt("dve", lb[:, :, ll, :], lb[:, :, ll - 1, :], raw[:, :, ll, :], ALU.add, [lb, raw], [lb])
            P.ts("dve", oml[:], lb[:], -1.0, 1.0, ALU.mult, ALU.add, [lb], [oml])
            if inner == 256:
                P.dma(LBS[0], lb[:], reads=[lb], writes=[LBS])
                P.dma(LBS[1], oml[:], reads=[oml], writes=[LBS])
            else:
                P.cp("dve", lbfm[:], lb[:], [lb], [lbfm])
                P.cp("dve", oml_fm[:], oml[:], [oml], [oml_fm])

    for l in range(L):
        need_ctx = not (last_global and l == DEPTH - 1)
        with P.scope():
            wa = [P.sb("wa", [128, 8, 768], F32) for _ in range(2)]
            mv = modv[:].rearrange("p m c r -> p (m c) r")
            for blk in range(8):
                w = wa[blk % 2]
                src = w_ada[l, :, blk * 768:(blk + 1) * 768].rearrange("(kc p) n -> p kc n", p=128)
                P.dma(w[:], src, reads=[w_ada], writes=[w])
                ps = P.bank()
                for j in range(6):
                    for kc in range(8):
                        P.mm(ps[:, j * 3:(j + 1) * 3], w[:, kc, j * 128:(j + 1) * 128], scT[:, kc, :],
                             kc == 0, kc == 7, [w, scT], [ps])
                P.tt("dve", mv[:, blk * 6:(blk + 1) * 6, :], ps[:, 0:18].rearrange("p (j r) -> p j r", r=3),
                     bada[:, l, blk * 6:(blk + 1) * 6].unsqueeze(2).to_broadcast([128, 6, 3]), ALU.add,
                     [ps, bada], [(modv, blk)])
            for (g, mi, nn) in ((g1, 1, n1), (g2, 4, n2)):
                P.ts("dve", g[:], modv[:, mi, :, :], 1.0, None, ALU.add, None, [modv], [g])
                P.tt("dve", g[:], g[:], nn[:, l, :].unsqueeze(2).to_broadcast([128, 8, 3]), ALU.mult, [g, nn], [g])

        mix_scope = P.scope()
        mix_scope.__enter__()
        hT = P.sb("hT", [128, 8, TT], BF16)
        yT = P.sb("yT", [128, 8, TT], BF16)
        P.memset("pool", yT[:].rearrange("p a t -> p (a t)"), 0.0, [yT])
        for b in range(2):
            with P.scope():
                xbs = [P.sb("xb", [128, 8, 512], F32) for _ in range(2)]
                sqs = [P.sb("sq", [128, 8, 512], BF16) for _ in range(2)]
                rss = [P.sb("rs", [128, 512], F32) for _ in range(2)]
                for bi, (c0, n, r) in enumerate(blocks512()):
                    r = b if r is None else r
                    xb, sq, rs = xbs[bi % 2], sqs[bi % 2], rss[bi % 2]
                    P.dma(xb[:, :, :n], XT[b, :, :, c0:c0 + n], reads=xk(b, c0, n), writes=[xb])
                    P.act(sq[:, :, :n], xb[:, :, :n], AF.Square, [xb], [sq])
                    ps = P.bank()
                    for c in range(8):
                        P.mm(ps[:, :n], ones_bf[:], sq[:, c, :n], c == 0, c == 7, [ones_bf, sq], [ps])
                    rstd_from_sumsq(ps[:, :n], rs[:, :n], 1.0 / D, ps, rs)
                    P.tt("dve", xb[:, :, :n], xb[:, :, :n], rs[:, :n].unsqueeze(1).to_broadcast([128, 8, n]), ALU.mult,
                         [xb, rs], [xb])
                    for c in range(8):
                        P.act(hT[:, c, c0:c0 + n], xb[:, c, :n], AF.Identity, [xb, g1, modv], [hT],
                              scale=g1[:, c, r:r + 1], bias=modv[:, 0, c, r:r + 1])
                if dbg and l == 0 and b == 0:
                    d_h = dbg_tensor("hT", [128, 8, TT])
                    hf = P.sb("hf", [128, 8, TT // 2], F32)
                    for hh in range(2):
                        P.cp("dve", hf[:], hT[:, :, hh * (TT // 2):(hh + 1) * (TT // 2)], [hT], [hf])
                        P.dma(d_h[:, :, hh * (TT // 2):(hh + 1) * (TT // 2)], hf[:], reads=[hf], writes=[d_h])

            if "swa" in groups:
                swa_group(l, b, need_ctx)
            if "hg" in groups:
                hg_group(l, b, need_ctx)
            if "dn" in groups:
                dn_group(l, b, need_ctx)

            if dbg and l == 0 and b == 0:
                d_y = dbg_tensor("yT", [128, 8, TT])
                with P.scope():
                    yf = P.sb("yf", [128, 8, TT // 2], F32)
                    for hh in range(2):
                        P.cp("dve", yf[:], yT[:, :, hh * (TT // 2):(hh + 1) * (TT // 2)], [yT], [yf])
                        P.dma(d_y[:, :, hh * (TT // 2):(hh + 1) * (TT // 2)], yf[:], reads=[yf], writes=[d_y])

            with P.scope():
                wout = P.sb("wout", [128, 8, D], BF16)
                load_w(wout, w_out[l], 8, w_out)
                xbs = [P.sb("xb", [128, 8, 512], F32) for _ in range(2)]
                for bi, (c0, n, r) in enumerate(blocks512()):
                    if r == 2 and not need_ctx:
                        continue
                    r = b if r is None else r
                    xb = xbs[bi % 2]
                    P.dma(xb[:, :, :n], XT[b, :, :, c0:c0 + n], reads=xk(b, c0, n), writes=[xb])
                    for c in range(8):
                        ps = P.bank()
                        for kc in range(8):
                            P.mm(ps[:, :n], wout[:, kc, c * 128:(c + 1) * 128], yT[:, kc, c0:c0 + n], kc == 0, kc == 7,
                                 [wout, yT], [ps])
                        P.stt(xb[:, c, :n], ps[:, :n], modv[:, 2, c, r:r + 1], xb[:, c, :n], ALU.mult, ALU.add,
                              [ps, modv, xb], [xb])
                    P.dma(XT[b, :, :, c0:c0 + n], xb[:, :, :n], reads=[xb], writes=xk(b, c0, n))

        mix_scope.__exit__(None, None, None)
        with P.scope():
            w1 = P.sb("w1", [128, 8, 4 * D], BF16)
            w2 = P.sb("w2", [128, 32, D], BF16)
            for cb in range(8):
                P.dma(w1[:, :, cb * 512:(cb + 1) * 512],
                      w_ff1[l][:, cb * 512:(cb + 1) * 512].rearrange("(kc p) n -> p kc n", p=128),
                      reads=[w_ff1], writes=[(w1, cb)], queue="pool")
            load_w(w2, w_ff2[l], 32, w_ff2)
            xbs = [P.sb("xb", [128, 8, 256], F32) for _ in range(2)]
            sqs = [P.sb("sq", [128, 8, 256], BF16) for _ in range(2)]
            rss = [P.sb("rs", [128, 256], F32) for _ in range(2)]
            tfs = [P.sb("tf", [128, 256], F32) for _ in range(2)]
            h2s = [P.sb("h2", [128, 8, 256], BF16) for _ in range(2)]
            Hs = [P.sb("H", [128, 32, 256], BF16) for _ in range(1)]
            rl = [P.sb("rl", [128, 256], BF16) for _ in range(3)]
            ot = [P.sb("ot", [128, D], F32) for _ in range(2)] if (last_global and l == DEPTH - 1) else None
            bi = 0
            for b in range(2):
                for blk in range(9):
                    c0, n = blk * 256, 256
                    r = 2 if blk == 0 else b
                    if blk == 0 and not need_ctx:
                        continue
                    xb, sq, rs, h2, H = xbs[bi % 2], sqs[bi % 2], rss[bi % 2], h2s[bi % 2], Hs[0]
                    bi += 1
                    P.dma(xb[:], XT[b, :, :, c0:c0 + n], reads=xk(b, c0, n), writes=[xb])
                    P.act(sq[:], xb[:], AF.Square, [xb], [sq])
                    ps = P.bank()
                    for c in range(8):
                        P.mm(ps[:, :n], ones_bf[:], sq[:, c, :], c == 0, c == 7, [ones_bf, sq], [ps])
                    rstd_from_sumsq(ps[:, :n], rs[:], 1.0 / D, ps, rs)
                    for c in range(8):
                        tf = tfs[c % 2]
                        P.stt(tf[:], xb[:, c, :], g2[:, c, r:r + 1], rs[:], ALU.mult, ALU.mult, [xb, g2, rs], [tf])
                        P.act(h2[:, c, :], tf[:], AF.Identity, [tf, modv], [h2], bias=modv[:, 3, c, r:r + 1])
                    for f in range(32):
                        ps = P.bank()
                        for kc in range(8):
                            P.mm(ps[:, :n], w1[:, kc, f * 128:(f + 1) * 128], h2[:, kc, :], kc == 0, kc == 7, [w1, h2], [ps])
                        t = rl[f % 3]
                        P.act(t[:], ps[:, :n], AF.Relu, [ps], [t])
                        P.tt("pool", H[:, f, :], t[:], t[:], ALU.mult, [t], [(H, f)])
                    for c in range(8):
                        ps = P.bank()
                        for f in range(32):
                            P.mm(ps[:, :n], w2[:, f, c * 128:(c + 1) * 128], H[:, f, :], f == 0, f == 31, [w2, H], [ps])
                        P.stt(xb[:, c, :], ps[:, :n], modv[:, 5, c, r:r + 1], xb[:, c, :], ALU.mult, ALU.add,
                              [ps, modv, xb], [xb])
                    if last_global and l == DEPTH - 1:
                        P.act(sq[:], xb[:], AF.Square, [xb], [sq])
                        ps = P.bank()
                        for c in range(8):
                            P.mm(ps[:, :n], ones_bf[:], sq[:, c, :], c == 0, c == 7, [ones_bf, sq], [ps])
                        rstd_from_sumsq(ps[:, :n], rs[:], 1.0 / D, ps, rs)
                        for c in range(8):
                            P.stt(xb[:, c, :], xb[:, c, :], nf[:, c:c + 1], rs[:], ALU.mult, ALU.mult, [xb, nf, rs], [xb])
                        for th in range(2):
                            o = ot[th]
                            for hb in range(2):
                                ps = P.bank()
                                for c4 in range(4):
                                    c = hb * 4 + c4
                                    P.tr(ps[:, c4 * 128:(c4 + 1) * 128], xb[:, c, th * 128:(th + 1) * 128], ident[:],
                                         [xb, ident], [ps])
                                P.cp("act" if hb else "dve", o[:, hb * 512:(hb + 1) * 512], ps[:], [ps], [(o, hb)])
                            t0 = c0 - T_CTX + th * 128
                            P.dma(out_d[b, t0:t0 + 128, :], o[:], reads=[o], writes=[(out_d, (b, t0))])
                    else:
                        P.dma(XT[b, :, :, c0:c0 + n], xb[:], reads=[xb], writes=xk(b, c0, n))

    if not last_global:
        with P.scope():
            xbs = [P.sb("xb", [128, 8, 128], F32) for _ in range(2)]
            ot = [P.sb("ot", [128, D], F32) for _ in range(2)]
            k = 0
            for b in range(2):
                for ti in range(2, NT):
                    xb, o = xbs[k % 2], ot[k % 2]
                    k += 1
                    P.dma(xb[:], XT[b, :, :, ti * 128:(ti + 1) * 128], reads=xk(b, ti * 128, 128), writes=[xb])
                    for hb in range(2):
                        ps = P.bank()
                        for c4 in range(4):
                            c = hb * 4 + c4
                            P.tr(ps[:, c4 * 128:(c4 + 1) * 128], xb[:, c, :], ident[:], [xb, ident], [ps])
                        P.cp("act" if hb else "dve", o[:, hb * 512:(hb + 1) * 512], ps[:], [ps], [(o, hb)])
                    P.dma(out_d[b, (ti - 2) * 128:(ti - 1) * 128, :], o[:], reads=[o], writes=[(out_d, (b, ti))])

    P.final_wait()
    P.emit()
    return nc, P, dbg_out


_SHARED_KEYS = ["w_ada", "b_ada", "norm1", "norm2", "norm_f", "w_dn", "w_sw", "w_hg", "w_out", "w_ff1", "w_ff2",
                "dn_conv", "dn_alog", "dn_dtb", "sink", "dn_norm", "hg_norm", "lb_rep", "lb_fm",
                "ident", "perm", "blk64", "cosT", "sinsT", "swa_prev", "swa_next", "dn_masks", "hg_a", "hg_last", "hg_mask", "ones64"]


def make_in_maps(inp, n_cores=8, WL=DEPTH):
    shared = _prep_shared(inp)
    for nm in ["w_ada", "w_dn", "w_sw", "w_hg", "w_out", "w_ff1", "w_ff2"]:
        shared[nm] = np.ascontiguousarray(shared[nm][:WL])
    x = np.asarray(inp["x"], np.float32)
    ctx = np.asarray(inp["ctx"], np.float32)
    c = np.asarray(inp["c"], np.float32)
    c_ctx = np.asarray(inp["c_ctx"], np.float32)
    maps = []
    for i in range(n_cores):
        m = {k: shared[k] for k in _SHARED_KEYS}
        m["x"] = np.ascontiguousarray(x[2 * i:2 * i + 2])
        m["ctx"] = np.ascontiguousarray(ctx[2 * i:2 * i + 2])
        rows = np.stack([c[2 * i], c[2 * i + 1], c_ctx], axis=1)
        m["cT"] = np.ascontiguousarray(rows.reshape(8, 128, 3).transpose(1, 0, 2))
        maps.append(m)
    return maps


def kernel(**inputs):
    inp = {k: np.asarray(v) for k, v in inputs.items()}
    nc, P, _ = build_program(DEPTH)
    maps = make_in_maps(inp, 8)
    res = run_bass_kernel_spmd(nc, maps, core_ids=list(range(8)))
    out = np.concatenate([np.asarray(r["out"], np.float32) for r in res.results], axis=0)
    return out
```
